# Optimizing a Trainium2 kernel written in Bass

```python
import jax, jax.numpy as jnp
from jax import lax
import numpy as np


D_MODEL = 2048
BATCH = 4
SEQ = 4096
DEPTH = 2

PLE_DIM = 256
EPS = 1e-6
NEG = -1e30
RET_DV = 256
RET_DK = 128
RET_HEADS = (D_MODEL // 2) // RET_DV
RET_CHUNK = 128
ATT_HD = 128
ATT_HEADS = (D_MODEL // 2) // ATT_HD
DILATED_PATTERNS = ((128, 1), (512, 4), (2048, 16))
ATT_BLOCK = 128
RET_WIDTH = RET_HEADS * RET_DV
ATT_WIDTH = ATT_HEADS * ATT_HD
MIX_WIDTH_EVEN = RET_WIDTH + ATT_WIDTH
IN_WIDTH_EVEN = 2 * RET_HEADS * RET_DK + RET_WIDTH + 3 * ATT_WIDTH + MIX_WIDTH_EVEN
CONV_WIDTH = 3
CONV_CH = D_MODEL
IN_WIDTH_ODD = 4 * CONV_CH
N_EVEN = (DEPTH + 1) // 2
N_ODD = DEPTH // 2

kernel_name = 'hybrid_retention_dilated_attn_shortconv'


def rms_norm(x, g):
    x32 = x.astype(jnp.float32)
    y = x32 * lax.rsqrt(jnp.mean(x32 * x32, axis=-1, keepdims=True) + EPS)
    return (y * g.astype(jnp.float32)).astype(x.dtype)


def alibi_slopes(n):
    start = 2.0 ** (-8.0 / n)
    return jnp.asarray(start ** np.arange(1, n + 1), dtype=jnp.float32)


def retention(q, k, v, gn_g):
    B, T, H, dk = q.shape
    dv = v.shape[-1]
    C = RET_CHUNK
    N = T // C
    f32 = jnp.float32
    log_g = jnp.asarray(np.log(1.0 - 2.0 ** (-5.0 - np.arange(H))), dtype=f32)
    pos = jnp.arange(C, dtype=f32)
    diff = pos[:, None] - pos[None, :]
    decay = jnp.where(diff >= 0, jnp.exp(log_g[:, None, None] * jnp.maximum(diff, 0.0)), 0.0)
    xi = jnp.exp(log_g[:, None] * (pos + 1.0))
    zeta = jnp.exp(log_g[:, None] * (C - 1.0 - pos))
    g_chunk = jnp.exp(log_g * C)
    qc = q.astype(f32).reshape(B, N, C, H, dk)
    kc = (k.astype(f32) * dk ** -0.5).reshape(B, N, C, H, dk)
    vc = v.astype(f32).reshape(B, N, C, H, dv)
    scores = jnp.einsum('bnchk,bnshk->bnhcs', qc, kc) * decay
    y_inner = jnp.einsum('bnhcs,bnshv->bnchv', scores, vc)
    contrib = jnp.einsum('bnshk,bnshv,hs->nbhkv', kc, vc, zeta)

    def step(R, S):
        return g_chunk[None, :, None, None] * R + S, R

    _, R_prev = lax.scan(step, jnp.zeros((B, H, dk, dv), f32), contrib)
    y_cross = jnp.einsum('bnchk,nbhkv,hc->bnchv', qc, R_prev, xi)
    y = (y_inner + y_cross).reshape(B, T, H, dv)
    mu = jnp.mean(y, axis=-1, keepdims=True)
    var = jnp.mean(jnp.square(y - mu), axis=-1, keepdims=True)
    y = (y - mu) * lax.rsqrt(var + EPS) * gn_g.astype(f32)
    return y.reshape(B, T, H * dv).astype(v.dtype)


def dilated_branch(q, k, v, window, dil, slopes):
    B, H, T, hd = q.shape
    steps = window // dil
    assert steps <= ATT_BLOCK
    L = T // dil
    nb = -(-L // ATT_BLOCK)
    Lp = nb * ATT_BLOCK

    def gather(a):
        a = a.reshape(B, H, L, dil, hd).transpose(0, 1, 3, 2, 4)
        a = jnp.pad(a, ((0, 0), (0, 0), (0, 0), (0, Lp - L), (0, 0)))
        return a.reshape(B, H, dil, nb, ATT_BLOCK, hd)

    def band(a):
        prev = jnp.pad(a[:, :, :, :-1], ((0, 0), (0, 0), (0, 0), (1, 0), (0, 0), (0, 0)))
        return jnp.concatenate([prev, a], axis=-2)

    qb = gather(q)
    kband = band(gather(k))
    vband = band(gather(v))
    s = jnp.einsum('bhrnqd,bhrnkd->bhrnqk', qb, kband)
    qi = jnp.arange(ATT_BLOCK)[:, None]
    ki = jnp.arange(2 * ATT_BLOCK)[None, :]
    rel = qi - ki + ATT_BLOCK
    blk = jnp.arange(nb)[:, None, None]
    valid = (rel >= 0) & (rel <= steps) & ((blk > 0) | (ki >= ATT_BLOCK))
    bias = -slopes[:, None, None] * (rel * dil).astype(jnp.float32)[None]
    s = jnp.where(valid[None, None, None], s + bias[None, :, None, None], NEG)
    m = jnp.max(s, axis=-1, keepdims=True)
    pr = jnp.exp(s - m)
    l = jnp.sum(pr, axis=-1)
    o = jnp.einsum('bhrnqk,bhrnkd->bhrnqd', pr, vband) / l[..., None]
    lse = m[..., 0] + jnp.log(l)
    o = o.reshape(B, H, dil, Lp, hd)[:, :, :, :L].transpose(0, 1, 3, 2, 4).reshape(B, H, T, hd)
    lse = lse.reshape(B, H, dil, Lp)[:, :, :, :L].transpose(0, 1, 3, 2).reshape(B, H, T)
    return o, lse


def dilated_attention(q, k, v, slopes):
    outs, lses = [], []
    for window, dil in DILATED_PATTERNS:
        o, lse = dilated_branch(q, k, v, window, dil, slopes)
        outs.append(o)
        lses.append(lse)
    w = jax.nn.softmax(jnp.stack(lses), axis=0)
    return jnp.einsum('gbht,gbhtd->bhtd', w, jnp.stack(outs))


def even_layer(h, w_in, q_g, k_g, gn_g, w_out, slopes):
    B, T, _ = h.shape
    rk_w = RET_HEADS * RET_DK
    offs = [rk_w, 2 * rk_w, 2 * rk_w + RET_WIDTH, 2 * rk_w + RET_WIDTH + ATT_WIDTH,
            2 * rk_w + RET_WIDTH + 2 * ATT_WIDTH, 2 * rk_w + RET_WIDTH + 3 * ATT_WIDTH]
    rq, rk, rv, aq, ak, av, z = jnp.split(h @ w_in, offs, axis=-1)
    y_ret = retention(rq.reshape(B, T, RET_HEADS, RET_DK), rk.reshape(B, T, RET_HEADS, RET_DK),
                      rv.reshape(B, T, RET_HEADS, RET_DV), gn_g)
    aq = rms_norm(aq.reshape(B, T, ATT_HEADS, ATT_HD), q_g)
    ak = rms_norm(ak.reshape(B, T, ATT_HEADS, ATT_HD), k_g)
    to_bhtd = lambda a: a.reshape(B, T, ATT_HEADS, ATT_HD).transpose(0, 2, 1, 3).astype(jnp.float32)
    y_att = dilated_attention(to_bhtd(aq) * ATT_HD ** -0.5, to_bhtd(ak), to_bhtd(av), slopes)
    y_att = y_att.transpose(0, 2, 1, 3).reshape(B, T, ATT_WIDTH).astype(h.dtype)
    y = jnp.concatenate([y_ret, y_att], axis=-1) * jax.nn.silu(z)
    return y @ w_out


def odd_layer(h, w_in, conv_w, w_out):
    bg, cg, u, z = jnp.split(h @ w_in, 4, axis=-1)
    u = cg * u
    T = u.shape[1]
    up = jnp.pad(u, ((0, 0), (CONV_WIDTH - 1, 0), (0, 0)))
    conv = sum(conv_w[j] * up[:, j:j + T] for j in range(CONV_WIDTH))
    return (bg * conv * jax.nn.silu(z)) @ w_out


def setup_inputs(seed: int = 0) -> dict:
    key = jax.random.key(seed)
    ks = jax.random.split(key, 14)
    f32 = jnp.float32

    def normal(k, shape, scale):
        return jax.random.normal(k, shape, f32) * scale

    def gain(k, shape):
        return 1.0 + 0.02 * jax.random.normal(k, shape, f32)

    return {
        'x': normal(ks[0], (BATCH, SEQ, D_MODEL), 1.0),
        'p': normal(ks[1], (DEPTH, BATCH, SEQ, PLE_DIM), 1.0),
        'pre_norm_g': gain(ks[2], (DEPTH, D_MODEL)),
        'w_in_even': normal(ks[3], (N_EVEN, D_MODEL, IN_WIDTH_EVEN), D_MODEL ** -0.5),
        'q_norm_g': gain(ks[4], (N_EVEN, ATT_HD)),
        'k_norm_g': gain(ks[5], (N_EVEN, ATT_HD)),
        'ret_gn_g': gain(ks[6], (N_EVEN, RET_HEADS, RET_DV)),
        'w_out_even': normal(ks[7], (N_EVEN, MIX_WIDTH_EVEN, D_MODEL), MIX_WIDTH_EVEN ** -0.5),
        'w_in_odd': normal(ks[8], (N_ODD, D_MODEL, IN_WIDTH_ODD), D_MODEL ** -0.5),
        'conv_w_odd': normal(ks[9], (N_ODD, CONV_WIDTH, CONV_CH), CONV_WIDTH ** -0.5),
        'w_out_odd': normal(ks[10], (N_ODD, CONV_CH, D_MODEL), CONV_CH ** -0.5),
        'ple_norm_g': gain(ks[11], (DEPTH, D_MODEL)),
        'w_ple_gate': normal(ks[12], (DEPTH, D_MODEL, D_MODEL), D_MODEL ** -0.5),
        'w_ple_proj': normal(ks[13], (DEPTH, PLE_DIM, D_MODEL), PLE_DIM ** -0.5),
    }


def reference(x, p, pre_norm_g, w_in_even, q_norm_g, k_norm_g, ret_gn_g, w_out_even,
              w_in_odd, conv_w_odd, w_out_odd, ple_norm_g, w_ple_gate, w_ple_proj):
    slopes = alibi_slopes(ATT_HEADS)
    for i in range(DEPTH):
        h = rms_norm(x, pre_norm_g[i])
        if i % 2 == 0:
            j = i // 2
            mix = even_layer(h, w_in_even[j], q_norm_g[j], k_norm_g[j], ret_gn_g[j], w_out_even[j], slopes)
        else:
            j = i // 2
            mix = odd_layer(h, w_in_odd[j], conv_w_odd[j], w_out_odd[j])
        x = x + mix
        gate = jax.nn.sigmoid(rms_norm(x, ple_norm_g[i]) @ w_ple_gate[i])
        x = x + gate * (p[i] @ w_ple_proj[i])
    return x
```

```python
import contextlib
import numpy as np
import ml_dtypes
import concourse.bass as bass
import concourse.mybir as mybir
from concourse.bass_utils import run_bass_kernel_spmd

F32 = mybir.dt.float32
BF16 = mybir.dt.bfloat16
AF = mybir.ActivationFunctionType
ALU = mybir.AluOpType

D = 2048
T = 4096
NT_ALL = 32
NT_CTX = 15
NT_OWN = 17
TOK_CTX = NT_CTX * 128
TOK_OWN = NT_OWN * 128
EPS = 1e-6
BW = 256
SAME_SYNC = True
ETW = 1920
PATTERNS = (1, 4, 16)
RET_GAMMA = [1.0 - 2.0 ** (-5.0 - h) for h in range(4)]


class Res:
    __slots__ = ("name", "w", "r")

    def __init__(self, name=""):
        self.name = name
        self.w = None
        self.r = {}


class Prog:
    def __init__(self, nc, ndma_sems=10):
        self.nc = nc
        self.eng = {"pe": nc.tensor, "dve": nc.vector, "act": nc.scalar, "pool": nc.gpsimd, "sp": nc.sync}
        self.sems = {}
        self.cnt = {}
        self._stack = []
        for e in self.eng:
            self.sems[e] = self._sem("s_" + e)
            self.cnt[e] = 0
        self.dring = {}
        for q in ("sp", "act", "pool"):
            ring = []
            for i in range(ndma_sems):
                k = "d_%s_%d" % (q, i)
                self.sems[k] = self._sem(k)
                self.cnt[k] = 0
                ring.append(k)
            self.dring[q] = [ring, 0]
        self.seen = {e: {} for e in self.eng}
        self.same_engine_sync = {"dve": SAME_SYNC, "act": SAME_SYNC, "pool": SAME_SYNC, "pe": False, "sp": False}

    def _sem(self, name):
        cm = self.nc.semaphore(name)
        h = cm.__enter__()
        self._stack.append(cm)
        return h

    def close(self):
        for cm in reversed(self._stack):
            cm.__exit__(None, None, None)

    def _wait(self, e, key, val):
        if val <= 0 or self.seen[e].get(key, 0) >= val:
            return
        self.eng[e].wait_ge(self.sems[key], val)
        self.seen[e][key] = val

    def _deps(self, e, reads, writes):
        for r in reads:
            if r.w is not None:
                k, v = r.w
                if k != e or self.same_engine_sync[e]:
                    self._wait(e, k, v)
        for w in writes:
            if w.w is not None:
                k, v = w.w
                if k != e or self.same_engine_sync[e]:
                    self._wait(e, k, v)
            for k, v in w.r.items():
                if k != e:
                    self._wait(e, k, v)

    def op(self, e, fn, reads=(), writes=()):
        self._deps(e, reads, writes)
        ins = fn()
        self.cnt[e] += 1
        ins.then_inc(self.sems[e], 1)
        v = self.cnt[e]
        for r in reads:
            if r.r.get(e, 0) < v:
                r.r[e] = v
        for w in writes:
            w.w = (e, v)
            w.r = {}
        return ins

    def dma(self, q, out, in_, reads=(), writes=(), **kw):
        ring, idx = self.dring[q]
        k = ring[idx % len(ring)]
        self.dring[q][1] = idx + 1
        self._wait(q, k, self.cnt[k])
        self._deps(q, reads, writes)
        ins = self.eng[q].dma_start(out=out, in_=in_, **kw)
        self.cnt[k] += 16
        ins.then_inc(self.sems[k], 16)
        v = self.cnt[k]
        for r in reads:
            r.r[k] = v
        for w in writes:
            w.w = (k, v)
            w.r = {}
        return ins

    def barrier(self):
        for e in self.eng:
            for k in self.sems:
                if k != e:
                    self._wait(e, k, self.cnt[k])


_UNIQ = [0]


def sbuf(nc, name, shape, dtype):
    _UNIQ[0] += 1
    return nc.sbuf_tensor("%s_%d" % (name, _UNIQ[0]), shape, dtype)


def psum(nc, name, shape, dtype):
    _UNIQ[0] += 1
    return nc.psum_tensor("%s_%d" % (name, _UNIQ[0]), shape, dtype)


class WRes:
    def __init__(self, lo, hi, hk):
        self.lo, self.hi, self.hk = lo, hi, hk

    def __getitem__(self, k):
        return self.lo if k < self.hk else self.hi


class Ring:
    def __init__(self, nc, stack, name, shape, dtype, n, psum=False):
        self.items = []
        for i in range(n):
            alloc = globals()["psum"] if psum else sbuf
            t = stack.enter_context(alloc(nc, "%s%d" % (name, i), shape, dtype))
            self.items.append((t, Res("%s%d" % (name, i))))
        self.i = 0

    def next(self):
        it = self.items[self.i % len(self.items)]
        self.i += 1
        return it


def att_items():
    items = []
    for pi, d in enumerate(PATTERNS):
        nb = T // d // 128
        for r in range(d):
            h0 = 1920 // d
            n = h0 // 128
            items.append((pi, d, r, n, h0 - n * 128, True))
            for n in range(2048 // d // 128, nb):
                items.append((pi, d, r, n, 0, False))
    return items


def build(dbg=False, upto=99):
    nc = bass.Bass("TRN2", target_bir_lowering=False)
    P = Prog(nc)

    def din(name, shape, dt=F32):
        return nc.dram_tensor(name, list(shape), dt, kind="ExternalInput").ap()

    def dscr(name, shape, dt=BF16):
        return nc.dram_tensor(name, list(shape), dt, kind="ExternalOutput" if dbg else "Internal").ap()

    xin = din("xin", [T, D])
    pin = din("pin", [2, TOK_OWN, 256])
    w_in0 = din("w_in0", [D, 7168])
    w_out0 = din("w_out0", [D, D])
    w_in1 = din("w_in1", [D, 8192])
    w_out1 = din("w_out1", [D, D])
    w_gate = din("w_gate", [2, D, D])
    w_ple = din("w_ple", [2, 256, D])
    gcol_d = din("gcol", [4, 128, 16])
    qkg = din("qkg", [128, 2])
    gng = din("gng", [128, 1024])
    convw = din("convw", [128, 16, 3])
    etab = din("etab", [8, 128, ETW])
    rtab = din("rtab", [128, 8, 128])
    zeta = din("zeta", [128, 4])
    ident_d = din("ident", [128, 128], BF16)
    ones_d = din("ones", [128, 128], BF16)
    out = nc.dram_tensor("out", [2048, D], F32, kind="ExternalOutput").ap()

    QT = dscr("QT", [8, 128, TOK_OWN])
    KT = dscr("KT", [8, 128, T])
    VT = dscr("VT", [8, 128, T])
    ZT = dscr("ZT", [8, 128, TOK_OWN])
    RQT = dscr("RQT", [4, 128, TOK_OWN])
    RKT = dscr("RKT", [4, 128, TOK_OWN])
    RKV = dscr("RKV", [NT_ALL, 128, 1536])
    ZR = dscr("ZR", [NT_OWN, 128, 1024])
    YT = dscr("YT", [16, 128, TOK_OWN])
    X1 = dscr("X1", [TOK_OWN, D], F32)
    X2 = dscr("X2", [TOK_OWN, D], F32)
    X3 = dscr("X3", [TOK_OWN, D], F32)

    gstack = contextlib.ExitStack()
    ident = gstack.enter_context(sbuf(nc, "ident_s", [128, 128], BF16))
    ones = gstack.enter_context(sbuf(nc, "ones_s", [128, 128], BF16))
    qkg_s = gstack.enter_context(sbuf(nc, "qkg_s", [128, 2], F32))
    qkg2 = gstack.enter_context(sbuf(nc, "qkg2", [128, 2], F32))
    r_const = Res("const")
    P.dma("sp", ident[:], ident_d[:, :], writes=[r_const])
    P.dma("sp", ones[:], ones_d[:, :], writes=[r_const])
    P.dma("sp", qkg_s[:], qkg[:, :], writes=[r_const])
    P.op("dve", lambda: nc.vector.tensor_scalar(out=qkg2[:, 0:1], in0=qkg_s[:, 0:1], scalar1=float(128 ** -0.5),
                                                 scalar2=None, op0=ALU.mult), reads=[r_const], writes=[r_const])
    P.op("dve", lambda: nc.vector.tensor_copy(out=qkg2[:, 1:2], in_=qkg_s[:, 1:2]), reads=[r_const], writes=[r_const])
    P.barrier()

    def rstd_from_sumsq(stack_pools, ss, ss_r, scale):
        (lnr, rsr) = stack_pools
        ln_t, ln_r = lnr.next()
        rs_t, rs_r = rsr.next()
        P.op("act", lambda: nc.scalar.activation(out=ln_t[:], in_=ss[:], func=AF.Ln, bias=float(EPS), scale=float(scale)),
             reads=[ss_r], writes=[ln_r])
        P.op("act", lambda: nc.scalar.activation(out=rs_t[:], in_=ln_t[:], func=AF.Exp, scale=-0.5),
             reads=[ln_r], writes=[rs_r])
        return rs_t, rs_r

    def norm_transpose(stack, src_rows, ntiles, gidx, actT, actT_r, extra=None, nx=4, nxs=4):
        xr = Ring(nc, stack, "xt", [128, D], F32, nx)
        xsr = Ring(nc, stack, "xs", [128, D], BF16, nxs)
        ssr = Ring(nc, stack, "ss", [128, 1], F32, 4)
        lnr = Ring(nc, stack, "lnv", [128, 1], F32, 4)
        rsr = Ring(nc, stack, "rsv", [128, 1], F32, 4)
        ptr = Ring(nc, stack, "ptr", [128, 1024], BF16, 2, psum=True)
        loaded_x = {}

        def load_x(t):
            if t < ntiles and t not in loaded_x:
                xt, xt_r = xr.next()
                P.dma("sp", xt[:], src_rows(t), writes=[xt_r])
                loaded_x[t] = (xt, xt_r)

        def stage1(t):
            for tt in range(t, t + nx - 1):
                load_x(tt)
            xt, xt_r = loaded_x.pop(t)
            xs, xs_r = xsr.next()
            ss, ss_r = ssr.next()
            P.op("act", lambda: nc.scalar.activation(out=xs[:], in_=xt[:], func=AF.Square, accum_out=ss[:]),
                 reads=[xt_r], writes=[xs_r, ss_r])
            rs, rs_r = rstd_from_sumsq((lnr, rsr), ss, ss_r, 1.0 / D)
            P.op("dve", lambda: nc.vector.tensor_scalar(out=xs[:], in0=xt[:], scalar1=rs[:, 0:1], scalar2=None, op0=ALU.mult),
                 reads=[xt_r, rs_r], writes=[xs_r])
            return xs, xs_r

        def stage2(t, xs, xs_r):
            for half in range(2):
                pt, pt_r = ptr.next()
                for j in range(8):
                    k = half * 8 + j
                    P.op("pe", lambda: nc.tensor.transpose(out=pt[:, j * 128:(j + 1) * 128],
                                                           in_=xs[:, k * 128:(k + 1) * 128], identity=ident[:]),
                         reads=[xs_r], writes=[pt_r])
                dst = actT[:, half * 8:(half + 1) * 8, t * 128:(t + 1) * 128]
                srcv = pt[:].rearrange("p (k t) -> p k t", k=8)
                if half == 0:
                    P.op("act", lambda: nc.scalar.copy(out=dst, in_=srcv), reads=[pt_r], writes=[actT_r])
                else:
                    P.op("dve", lambda: nc.vector.tensor_copy(out=dst, in_=srcv), reads=[pt_r], writes=[actT_r])
            if extra is not None:
                extra(t)

        pend = {0: stage1(0)}
        if ntiles > 1:
            pend[1] = stage1(1)
        for t in range(ntiles):
            if t + 2 < ntiles:
                pend[t + 2] = stage1(t + 2)
            xs, xs_r = pend.pop(t)
            stage2(t, xs, xs_r)

    class WStream:
        def __init__(self, stack, kc=16, bw=BW, nbuf=2, name="w", nstg=2, gidx=None):
            self.gc = None
            if gidx is not None:
                self.gc = stack.enter_context(sbuf(nc, "gcol_s", [128, 16], F32))
                self.gc_r = Res("gcol")
                P.dma("sp", self.gc[:], gcol_d[gidx, :, :], writes=[self.gc_r])
            self.kc = kc
            self.bw = bw
            self.stg = Ring(nc, stack, name + "stg", [128, kc, bw], F32, nstg)
            self.wb = Ring(nc, stack, name + "wb", [128, kc, bw], BF16, nbuf)
            self.wres = {id(r): (Res("wlo"), Res("whi")) for (_, r) in self.wb.items}
            self.pending = []

        def prefetch(self, srcs):
            st, st_r = self.stg.next()
            for ap, c0, n in srcs:
                P.dma("sp", st[:, :, c0:c0 + n], ap, writes=[st_r])
            self.pending.append((st, st_r))

        def mid(self):
            if getattr(self, "want", False) and self.nxt is None:
                self.nxt = self.get()

        def get(self):
            st, st_r = self.pending.pop(0)
            wb, wb_r0 = self.wb.next()
            hk = self.kc // 2
            r_lo, r_hi = self.wres[id(wb_r0)]
            if self.gc is None:
                P.op("dve", lambda: nc.vector.tensor_copy(out=wb[:, 0:hk, :], in_=st[:, 0:hk, :]), reads=[st_r], writes=[r_lo])
                P.op("act", lambda: nc.scalar.copy(out=wb[:, hk:, :], in_=st[:, hk:, :]), reads=[st_r], writes=[r_hi])
            else:
                gc, gc_r = self.gc, self.gc_r
                for k in range(hk):
                    P.op("dve", lambda: nc.vector.tensor_scalar(out=wb[:, k, :], in0=st[:, k, :], scalar1=gc[:, k:k + 1],
                                                                scalar2=None, op0=ALU.mult), reads=[st_r, gc_r], writes=[r_lo])
                    k2 = hk + k
                    P.op("act", lambda: nc.scalar.activation(out=wb[:, k2, :], in_=st[:, k2, :], func=AF.Copy,
                                                             scale=gc[:, k2:k2 + 1]), reads=[st_r, gc_r], writes=[r_hi])
            return wb, WRes(r_lo, r_hi, hk)

    def wsrc(w2d, c0, n, kc=16):
        return w2d.rearrange("(k p) c -> p k c", p=128)[:, :, c0:c0 + n]

    def run_blocks(ws, blocks, body, pre=False):
        if not blocks:
            return
        if not pre:
            ws.prefetch(blocks[0])
        cur = ws.get()
        for i in range(len(blocks)):
            has_next = i + 1 < len(blocks)
            if has_next:
                ws.prefetch(blocks[i + 1])
            ws.nxt = None
            ws.want = has_next
            body(i, cur[0], cur[1])
            ws.mid()
            cur = ws.nxt

    RS = {}
    RKVD = [[Res("rkvd") for _ in range(3)] for _ in range(NT_ALL)]

    def rs_open():
        RS["stack"] = st = contextlib.ExitStack()
        RS["zt"] = st.enter_context(sbuf(nc, "zeta_s", [128, 4], F32))
        RS["zt_r"] = Res("zeta")
        P.dma("sp", RS["zt"][:], zeta[:, :], writes=[RS["zt_r"]])
        RS["R"] = st.enter_context(sbuf(nc, "Rst", [128, 4, 256], F32))
        RS["RbS"] = st.enter_context(sbuf(nc, "RbS", [128, NT_OWN, 4, 256], BF16))
        RS["R_r"] = [Res("R%d" % h) for h in range(4)]
        RS["snap_r"] = [[Res("snap") for h in range(4)] for j in range(NT_OWN)]
        for h in range(4):
            P.op("dve", lambda: nc.vector.memset(RS["R"][:, h, :], 0.0), writes=[RS["R_r"][h]])
        RS["next_t"] = 0

    def rs_close():
        RS["stack"].close()

    def pass1_tiles(n, rkvr, kzr, S_pr):
        R, RbS, R_r, snap_r, zt, zt_r = RS["R"], RS["RbS"], RS["R_r"], RS["snap_r"], RS["zt"], RS["zt_r"]
        for _ in range(n):
            t = RS["next_t"]
            if t >= NT_ALL - 1:
                return
            RS["next_t"] = t + 1
            rkv, rkv_r = rkvr.next()
            P.dma("sp", rkv[:], RKV[t, :, :], reads=(RKVD[t] if t >= NT_CTX else []), writes=[rkv_r])
            j = t - NT_CTX
            for h in range(4):
                if j >= 0:
                    P.op("act", lambda: nc.scalar.copy(out=RbS[:, j, h, :], in_=R[:, h, :]), reads=[R_r[h]], writes=[snap_r[j][h]])
                kz, kz_r = kzr.next()
                P.op("act", lambda: nc.scalar.activation(out=kz[:], in_=rkv[:, h * 128:(h + 1) * 128], func=AF.Copy,
                                                         scale=zt[:, h:h + 1]), reads=[rkv_r, zt_r], writes=[kz_r])
                Sp, Sp_r = S_pr.next()
                P.op("pe", lambda: nc.tensor.matmul(Sp[:], lhsT=kz[:], rhs=rkv[:, 512 + h * 256:512 + (h + 1) * 256],
                                                    start=True, stop=True), reads=[kz_r, rkv_r], writes=[Sp_r])
                gch = float(RET_GAMMA[h] ** 128)
                P.op("dve", lambda: nc.vector.scalar_tensor_tensor(out=R[:, h, :], in0=R[:, h, :], scalar=gch, in1=Sp[:],
                                                                   op0=ALU.mult, op1=ALU.add),
                     reads=[Sp_r, R_r[h]], writes=[R_r[h]])
            if t == NT_ALL - 2:
                for h in range(4):
                    jj = NT_OWN - 1
                    P.op("act", lambda: nc.scalar.copy(out=RbS[:, jj, h, :], in_=R[:, h, :]), reads=[R_r[h]],
                         writes=[snap_r[jj][h]])

    def phase_inproj0(tile0, ntiles, own):
        ntok = ntiles * 128
        stack = contextlib.ExitStack()
        actT = stack.enter_context(sbuf(nc, "actT", [128, 16, TOK_OWN], BF16))
        actT_r = Res("actT")
        ws = WStream(stack, bw=512, nstg=1, gidx=0)
        first_col = 512 if own else 3072
        ws.prefetch([(wsrc(w_in0, first_col, 512), 0, 512)])
        with contextlib.ExitStack() as s1:
            norm_transpose(s1, lambda t: xin[(tile0 + t) * 128:(tile0 + t + 1) * 128, :], ntiles, 0, actT, actT_r,
                           nx=(3 if own else 4), nxs=(3 if own else 4))
        P.barrier()
        if own:
            p1_rkvr = Ring(nc, stack, "p1rkv", [128, 1536], BF16, 3)
            p1_kzr = Ring(nc, stack, "p1kz", [128, 128], BF16, 4)
            p1_S = Ring(nc, stack, "p1S", [128, 256], F32, 2, psum=True)
        psr = Ring(nc, stack, "ps", [128, 512], F32, 4, psum=True)
        ps2r = Ring(nc, stack, "ps2", [128, 512], F32, 2, psum=True)
        stgr = Ring(nc, stack, "ostg", [128, 512], BF16, 4)
        sqr = Ring(nc, stack, "sq", [128, 512], BF16, 2)
        lnr = Ring(nc, stack, "lnw", [128, 512], F32, 2)
        rsr = Ring(nc, stack, "rsw", [128, 512], F32, 2)
        if own:
            groups = [(0, 128)] + [(128 + 512 * g, 512) for g in range(4)]
        else:
            groups = [(0, 512), (512, 512), (1024, 512), (1536, 384)]
        tok_off_all = tile0 * 128

        fm = []
        if own:
            for h in range(4):
                fm.append((0 + h * 128, "copy", RQT, h, 0, None))
            for h in range(4):
                fm.append((512 + h * 128, "copy", RKT, h, 0, None))
            for h in range(8):
                fm.append((2048 + h * 128, "qk", QT, h, 0, 0))
        for h in range(8):
            fm.append((3072 + h * 128, "qk", KT, h, tok_off_all, 1))
        for h in range(8):
            fm.append((4096 + h * 128, "copy", VT, h, tok_off_all, None))
        if own:
            for h in range(8):
                fm.append((6144 + h * 128, "silu", ZT, h, 0, None))
        fm_blocks = [fm[i:i + 4] for i in range(0, len(fm), 4)]

        deferred = []

        def flush():
            while deferred:
                deferred.pop(0)()

        def fm_body(i, wb, wb_r):
            for ci, (col0, kind, dst, di, toff, gi) in enumerate(fm_blocks[i]):
                for (g0, gn) in groups:
                    fm_group(wb, wb_r, ci, kind, dst, di, toff, gi, g0, gn)
                if ci == 1:
                    ws.mid()
            flush()

        def fm_group(wb, wb_r, ci, kind, dst, di, toff, gi, g0, gn):
                    ps, ps_r = psr.next()
                    for k in range(16):
                        P.op("pe", lambda: nc.tensor.matmul(ps[:, 0:gn], lhsT=wb[:, k, ci * 128:(ci + 1) * 128],
                                                            rhs=actT[:, k, g0:g0 + gn], start=(k == 0), stop=(k == 15)),
                             reads=[wb_r[k], actT_r], writes=[ps_r])
                    og, og_r = stgr.next()
                    if kind == "copy":
                        P.op("act", lambda: nc.scalar.copy(out=og[:, 0:gn], in_=ps[:, 0:gn]), reads=[ps_r], writes=[og_r])
                    elif kind == "silu":
                        P.op("act", lambda: nc.scalar.activation(out=og[:, 0:gn], in_=ps[:, 0:gn], func=AF.Silu),
                             reads=[ps_r], writes=[og_r])
                    else:
                        sq, sq_r = sqr.next()
                        P.op("act", lambda: nc.scalar.activation(out=sq[:, 0:gn], in_=ps[:, 0:gn], func=AF.Square),
                             reads=[ps_r], writes=[sq_r])
                        flush()
                        deferred.append(lambda: qk_tail(ps, ps_r, sq, sq_r, og, og_r, gn, gi, dst, di, toff, g0))
                        return
                    flush()
                    P.dma("sp", dst[di, :, toff + g0:toff + g0 + gn], og[:, 0:gn], reads=[og_r])

        def qk_tail(ps, ps_r, sq, sq_r, og, og_r, gn, gi, dst, di, toff, g0):
                        p2, p2_r = ps2r.next()
                        P.op("pe", lambda: nc.tensor.matmul(p2[:, 0:gn], lhsT=ones[:], rhs=sq[:, 0:gn], start=True, stop=True),
                             reads=[sq_r, r_const], writes=[p2_r])
                        ln_t, ln_r = lnr.next()
                        P.op("act", lambda: nc.scalar.activation(out=ln_t[:, 0:gn], in_=p2[:, 0:gn], func=AF.Ln,
                                                                 bias=float(EPS), scale=1.0 / 128),
                             reads=[p2_r], writes=[ln_r])
                        rs_t, rs_r = rsr.next()
                        P.op("act", lambda: nc.scalar.activation(out=rs_t[:, 0:gn], in_=ln_t[:, 0:gn], func=AF.Exp, scale=-0.5),
                             reads=[ln_r], writes=[rs_r])
                        P.op("dve", lambda: nc.vector.scalar_tensor_tensor(out=og[:, 0:gn], in0=ps[:, 0:gn],
                                                                           scalar=qkg2[:, gi:gi + 1], in1=rs_t[:, 0:gn],
                                                                           op0=ALU.mult, op1=ALU.mult),
                             reads=[ps_r, rs_r, r_const], writes=[og_r])
                        P.dma("sp", dst[di, :, toff + g0:toff + g0 + gn], og[:, 0:gn], reads=[og_r])

        tm = [(512 + j * 512, "copy", j * 512) for j in range(3)]
        if own:
            tm += [(5120 + j * 512, "silu", j * 512) for j in range(2)]

        def tm_body(i, wb, wb_r):
            col0, kind, dcol = tm[i]
            for t in range(ntiles):
                if t == ntiles // 2:
                    ws.mid()
                ps, ps_r = psr.next()
                for k in range(16):
                    P.op("pe", lambda: nc.tensor.matmul(ps[:], lhsT=actT[:, k, t * 128:(t + 1) * 128], rhs=wb[:, k, :],
                                                        start=(k == 0), stop=(k == 15)),
                         reads=[wb_r[k], actT_r], writes=[ps_r])
                og, og_r = stgr.next()
                fn = AF.Copy if kind == "copy" else AF.Silu
                P.op("act", lambda: nc.scalar.activation(out=og[:], in_=ps[:], func=fn), reads=[ps_r], writes=[og_r])
                if kind == "copy":
                    dstap = RKV[tile0 + t, :, dcol:dcol + 512]
                    wr = [RKVD[tile0 + t][i]]
                else:
                    dstap = ZR[t, :, dcol:dcol + 512]
                    wr = []
                P.dma("sp", dstap, og[:], reads=[og_r], writes=wr)

        nfm = len(fm_blocks)
        fm_src = [[(wsrc(w_in0, blk[0][0], 512), 0, 512)] for blk in fm_blocks]
        tm_src = [[(wsrc(w_in0, c0, 512), 0, 512)] for (c0, _, _) in tm]
        if own:
            order = [("tm", 0), ("tm", 1), ("tm", 2)] + [("fm", b) for b in range(nfm)] + [("tm", 3), ("tm", 4)]
        else:
            order = [("fm", b) for b in range(nfm)] + [("tm", b) for b in range(len(tm))]
        all_blocks = [(tm_src[b] if kind == "tm" else fm_src[b]) for (kind, b) in order]

        def body_all(i, wb, wb_r):
            kind, b = order[i]
            if kind == "tm":
                tm_body(b, wb, wb_r)
            else:
                fm_body(b, wb, wb_r)
            if own:
                if i < 3:
                    pass1_tiles(5, p1_rkvr, p1_kzr, p1_S)
                else:
                    pass1_tiles(2, p1_rkvr, p1_kzr, p1_S)

        run_blocks(ws, all_blocks, body_all, pre=True)
        if own:
            pass1_tiles(NT_ALL, p1_rkvr, p1_kzr, p1_S)
        P.barrier()
        stack.close()

    def phase_attention():
        stack = contextlib.ExitStack()
        qr = Ring(nc, stack, "qT", [128, TOK_OWN], BF16, 2)
        kr = Ring(nc, stack, "kT", [128, T], BF16, 2)
        vr = Ring(nc, stack, "vT", [128, T], BF16, 2)
        zr = Ring(nc, stack, "zT", [128, TOK_OWN], BF16, 2)
        er = Ring(nc, stack, "et", [128, ETW], F32, 2)
        vkeys = []
        for pi, d in enumerate(PATTERNS):
            nb = T // d // 128
            h0 = 1920 // d
            nh = h0 // 128
            n_first = 2048 // d // 128
            for r in range(d):
                blks = set()
                for n in [nh] + list(range(n_first, nb)):
                    blks.add(n)
                    if n >= 1:
                        blks.add(n - 1)
                for kb in sorted(blks):
                    vkeys.append((pi, r, kb))
        vslot = {k: i for i, k in enumerate(vkeys)}
        nvb = len(vkeys)
        vtok = stack.enter_context(sbuf(nc, "vtok", [128, nvb, 128], BF16))
        vtok_r = Res("vtok")
        p16 = stack.enter_context(sbuf(nc, "p16", [128, 16, 256], BF16))
        p16_r = Res("p16")
        yT, yT_r = YTS["t"], YTS["r"]
        ptr = Ring(nc, stack, "aptr", [128, 1024], BF16, 1, psum=True)
        spr = Ring(nc, stack, "sps", [128, 512], F32, 3, psum=True)
        obr = Ring(nc, stack, "obank", [128, 512], F32, 2, psum=True)
        dbr = Ring(nc, stack, "dbank", [128, 512], F32, 2, psum=True)
        pxr = Ring(nc, stack, "pexp", [128, 256], BF16, 3)
        pbr = Ring(nc, stack, "pT", [128, 256], BF16, 4)
        lnr = Ring(nc, stack, "lnd", [128, 512], F32, 1)
        rdr = Ring(nc, stack, "rd", [128, 512], F32, 1)
        y1r = Ring(nc, stack, "y1", [128, 512], F32, 1)

        def blk_slice(d, r, n, c0=0, off=0, cnt=None):
            start = r + d * (128 * n + c0) - off
            if cnt is None:
                cnt = 128 - c0
            return slice(start, start + d * (cnt - 1) + 1, d)

        def head_loads(hh):
            qT, q_r = qr.next()
            kT, k_r = kr.next()
            vT, v_r = vr.next()
            zT, z_r = zr.next()
            et, e_r = er.next()
            P.dma("sp", vT[:], VT[hh, :, :], writes=[v_r])
            P.dma("sp", kT[:], KT[hh, :, :], writes=[k_r])
            P.dma("sp", qT[:], QT[hh, :, :], writes=[q_r])
            P.dma("sp", et[:], etab[hh, :, :], writes=[e_r])
            P.dma("sp", zT[:], ZT[hh, :, :], writes=[z_r])
            return (qT, q_r, kT, k_r, vT, v_r, zT, z_r, et, e_r)

        hl = {0: head_loads(0)}
        for hh in range(8):
            (qT, q_r, kT, k_r, vT, v_r, zT, z_r, et, e_r) = hl.pop(hh)
            if hh + 1 < 8:
                hl[hh + 1] = head_loads(hh + 1)
            for b0 in range(0, nvb, 8):
                nb_ = min(8, nvb - b0)
                pt, pt_r = ptr.next()
                for j in range(nb_):
                    pi, r, kb = vkeys[b0 + j]
                    d = PATTERNS[pi]
                    P.op("pe", lambda: nc.tensor.transpose(out=pt[:, j * 128:(j + 1) * 128],
                                                           in_=vT[:, blk_slice(d, r, kb)], identity=ident[:]),
                         reads=[v_r], writes=[pt_r])
                eng = "act" if (b0 // 8) % 2 == 0 else "dve"
                dst = vtok[:, b0:b0 + nb_, :]
                srcv = pt[:, 0:nb_ * 128].rearrange("p (k t) -> p k t", k=nb_)
                if eng == "act":
                    P.op("act", lambda: nc.scalar.copy(out=dst, in_=srcv), reads=[pt_r], writes=[vtok_r])
                else:
                    P.op("dve", lambda: nc.vector.tensor_copy(out=dst, in_=srcv), reads=[pt_r], writes=[vtok_r])

            def make_p(pi, d, r, n, ctx, dst, dst_r):
                qs = blk_slice(d, r, n, 0, off=TOK_CTX)
                sp, sp_r = spr.next()
                for si, kb in enumerate((n - 1, n)):
                    P.op("pe", lambda: nc.tensor.matmul(sp[:, si * 128:(si + 1) * 128], lhsT=kT[:, blk_slice(d, r, kb)],
                                                        rhs=qT[:, qs], start=True, stop=True),
                         reads=[k_r, q_r], writes=[sp_r])
                px, px_r = pxr.next()
                P.op("act", lambda: nc.scalar.activation(out=px[:], in_=sp[:, 0:256], func=AF.Exp), reads=[sp_r], writes=[px_r])
                t0 = pi * 512 + (256 if ctx else 0)
                P.op("dve", lambda: nc.vector.tensor_tensor(out=dst, in0=px[:], in1=et[:, t0:t0 + 256], op=ALU.mult),
                     reads=[px_r, e_r], writes=[dst_r])

            for r in range(16):
                make_p(2, 16, r, 1, True, p16[:, r, :], p16_r)

            state = {}

            def acc(ob, ob_r, db, db_r, out_sl, lhs_slot, rhs, rhs_r):
                first = state["first"]
                P.op("pe", lambda: nc.tensor.matmul(ob[:, out_sl], lhsT=vtok[:, lhs_slot, :], rhs=rhs, start=first, stop=False,
                                                    skip_group_check=True),
                     reads=[vtok_r, rhs_r], writes=[ob_r])
                P.op("pe", lambda: nc.tensor.matmul(db[:, out_sl], lhsT=ones[:], rhs=rhs, start=first, stop=False,
                                                    skip_group_check=True),
                     reads=[rhs_r, r_const], writes=[db_r])
                state["first"] = False

            seq = []
            cur = {}

            def begin_group():
                cur["ob"], cur["ob_r"] = obr.next()
                cur["db"], cur["db_r"] = dbr.next()
                state["first"] = True

            def acc2(out_sl, lhs_slot, rhs, rhs_r):
                acc(cur["ob"], cur["ob_r"], cur["db"], cur["db_r"], out_sl, lhs_slot, rhs, rhs_r)

            def end_group(tok0, gw):
                ob, ob_r, db, db_r = cur["ob"], cur["ob_r"], cur["db"], cur["db_r"]
                ln_t, ln_r = lnr.next()
                P.op("act", lambda: nc.scalar.activation(out=ln_t[:, 0:gw], in_=db[:, 0:gw], func=AF.Ln), reads=[db_r], writes=[ln_r])
                rd, rd_r = rdr.next()
                P.op("act", lambda: nc.scalar.activation(out=rd[:, 0:gw], in_=ln_t[:, 0:gw], func=AF.Exp, scale=-1.0),
                     reads=[ln_r], writes=[rd_r])
                y1, y1_r = y1r.next()
                P.op("dve", lambda: nc.vector.tensor_tensor(out=y1[:, 0:gw], in0=ob[:, 0:gw], in1=rd[:, 0:gw], op=ALU.mult),
                     reads=[ob_r, rd_r], writes=[y1_r])
                P.op("dve", lambda: nc.vector.tensor_tensor(out=yT[:, 8 + hh, tok0:tok0 + gw], in0=y1[:, 0:gw],
                                                            in1=zT[:, tok0:tok0 + gw], op=ALU.mult),
                     reads=[y1_r, z_r], writes=[yT_r])

            def std_job(pi, d, r, n, ctx, out_sl):
                def pfn():
                    pb, pb_r = pbr.next()
                    make_p(pi, d, r, n, ctx, pb[:], pb_r)
                    return pb, pb_r

                def afn(pb, pb_r):
                    for si, kb in enumerate((n - 1, n)):
                        acc2(out_sl, vslot[(pi, r, kb)], pb[:, si * 128:(si + 1) * 128], pb_r)
                return ("job", pfn, afn)

            def halo4_p():
                sp, sp_r = spr.next()
                for r in range(4):
                    qs = blk_slice(4, r, 3, 96, off=TOK_CTX)
                    for si, kb in enumerate((2, 3)):
                        c = (r * 2 + si) * 32
                        P.op("pe", lambda: nc.tensor.matmul(sp[:, c:c + 32], lhsT=kT[:, blk_slice(4, r, kb)], rhs=qT[:, qs],
                                                            start=True, stop=True), reads=[k_r, q_r], writes=[sp_r])
                px, px_r = pxr.next()
                P.op("act", lambda: nc.scalar.activation(out=px[:], in_=sp[:, 0:256], func=AF.Exp), reads=[sp_r], writes=[px_r])
                pb, pb_r = pbr.next()
                P.op("dve", lambda: nc.vector.tensor_tensor(out=pb[:], in0=px[:], in1=et[:, 1536:1792], op=ALU.mult),
                     reads=[px_r, e_r], writes=[pb_r])
                return pb, pb_r

            def halo4_a(pb, pb_r):
                for r in range(4):
                    for si, kb in enumerate((2, 3)):
                        c = (r * 2 + si) * 32
                        acc2(slice(r, r + 4 * 31 + 1, 4), vslot[(1, r, kb)], pb[:, c:c + 32], pb_r)

            def halo16_p():
                sp, sp_r = spr.next()
                for r in range(16):
                    qs = blk_slice(16, r, 0, 120, off=TOK_CTX)
                    P.op("pe", lambda: nc.tensor.matmul(sp[:, r * 8:(r + 1) * 8], lhsT=kT[:, blk_slice(16, r, 0)], rhs=qT[:, qs],
                                                        start=True, stop=True), reads=[k_r, q_r], writes=[sp_r])
                px, px_r = pxr.next()
                P.op("act", lambda: nc.scalar.activation(out=px[:, 0:128], in_=sp[:, 0:128], func=AF.Exp),
                     reads=[sp_r], writes=[px_r])
                pb, pb_r = pbr.next()
                P.op("dve", lambda: nc.vector.tensor_tensor(out=pb[:, 0:128], in0=px[:, 0:128], in1=et[:, 1792:1920], op=ALU.mult),
                     reads=[px_r, e_r], writes=[pb_r])
                return pb, pb_r

            def halo16_a(pb, pb_r):
                for r in range(16):
                    acc2(slice(r, r + 16 * 7 + 1, 16), vslot[(2, r, 0)], pb[:, r * 8:(r + 1) * 8], pb_r)

            def d16_accs(g):
                def fn():
                    for r in range(16):
                        for si, kb in enumerate((0, 1)):
                            c = si * 128 + 32 * (g - 1)
                            acc2(slice(r, r + 16 * 31 + 1, 16), vslot[(2, r, kb)], p16[:, r, c:c + 32], p16_r)
                return fn

            for g in range(5):
                seq.append(("call", begin_group))
                if g == 0:
                    seq.append(std_job(0, 1, 0, 15, False, slice(0, 128)))
                    seq.append(("job", halo4_p, halo4_a))
                    seq.append(("job", halo16_p, halo16_a))
                    seq.append(("call", lambda: end_group(0, 128)))
                else:
                    seq.append(("call", d16_accs(g)))
                    for j in range(4):
                        n = 16 + 4 * (g - 1) + j
                        seq.append(std_job(0, 1, 0, n, n == 16, slice(j * 128, (j + 1) * 128)))
                    for r in range(4):
                        n = 4 + (g - 1)
                        seq.append(std_job(1, 4, r, n, n == 4, slice(r, r + 4 * 127 + 1, 4)))
                    seq.append(("call", (lambda g=g: end_group(128 + 512 * (g - 1), 512))))
            jobs = [e for e in seq if e[0] == "job"]
            LOOK = 2
            results = {}
            for i in range(min(LOOK, len(jobs))):
                results[i] = jobs[i][1]()
            ji = 0
            for e in seq:
                if e[0] == "call":
                    e[1]()
                else:
                    if ji + LOOK < len(jobs):
                        results[ji + LOOK] = jobs[ji + LOOK][1]()
                    pb, pb_r = results.pop(ji)
                    e[2](pb, pb_r)
                    ji += 1
        P.barrier()
        stack.close()

    def phase_retention():
        stack = contextlib.ExitStack()
        rt = stack.enter_context(sbuf(nc, "rtab_s", [128, 8, 128], F32))
        gn = stack.enter_context(sbuf(nc, "gng_s", [128, 1024], F32))
        tab_r = Res("rtabs")
        P.dma("sp", rt[:], rtab[:, :, :], writes=[tab_r])
        P.dma("sp", gn[:], gng[:, :], writes=[tab_r])
        RbS, snap_r = RS["RbS"], RS["snap_r"]
        rkvr = Ring(nc, stack, "rkv", [128, 1536], BF16, 3)
        rqr = Ring(nc, stack, "rq", [128, 4, 128], BF16, 3)
        rkr = Ring(nc, stack, "rk", [128, 4, 128], BF16, 3)
        zrr = Ring(nc, stack, "zr", [128, 1024], BF16, 3)
        gzr = Ring(nc, stack, "gz", [128, 1024], F32, 6)
        scr = Ring(nc, stack, "scT", [128, 4, 128], BF16, 3)
        qxr = Ring(nc, stack, "qxi", [128, 4, 128], BF16, 3)
        ysbr = Ring(nc, stack, "ysb", [128, 1024], F32, 6)
        ynr = Ring(nc, stack, "yn", [128, 1024], F32, 2)
        yretr = Ring(nc, stack, "yret", [128, 1024], BF16, 2)
        str_ = Ring(nc, stack, "bst", [128, 4, 6], F32, 3)
        mvr = Ring(nc, stack, "bmv", [128, 4, 2], F32, 6)
        lnr = Ring(nc, stack, "rln", [128, 4], F32, 4)
        rsr = Ring(nc, stack, "rrs", [128, 4], F32, 5)
        nmr_ = Ring(nc, stack, "nmr", [128, 4], F32, 4)
        sc_pr = Ring(nc, stack, "scps", [128, 512], F32, 2, psum=True)
        yA_pr = Ring(nc, stack, "ypsA", [128, 512], F32, 2, psum=True)
        yB_pr = Ring(nc, stack, "ypsB", [128, 512], F32, 2, psum=True)
        t_pr = Ring(nc, stack, "rtp", [128, 1024], BF16, 2, psum=True)
        C = [dict() for _ in range(NT_OWN)]

        def stA(j):
            c = C[j]
            rkv, rkv_r = rkvr.next()
            P.dma("sp", rkv[:], RKV[NT_CTX + j, :, :], writes=[rkv_r])
            rq, rq_r = rqr.next()
            P.dma("sp", rq[:], RQT[:, :, j * 128:(j + 1) * 128].rearrange("h p t -> p h t"), writes=[rq_r])
            rk, rk_r = rkr.next()
            P.dma("sp", rk[:], RKT[:, :, j * 128:(j + 1) * 128].rearrange("h p t -> p h t"), writes=[rk_r])
            zr_t, zr_r = zrr.next()
            P.dma("sp", zr_t[:], ZR[j, :, :], writes=[zr_r])
            c["gz"], c["gz_r"] = gzr.next()
            P.op("pool", lambda: nc.gpsimd.tensor_tensor(out=c["gz"][:], in0=zr_t[:], in1=gn[:], op=ALU.mult),
                 reads=[zr_r, tab_r], writes=[c["gz_r"]])
            scp, scp_r = sc_pr.next()
            for h in range(4):
                P.op("pe", lambda: nc.tensor.matmul(scp[:, h * 128:(h + 1) * 128], lhsT=rk[:, h, :], rhs=rq[:, h, :],
                                                    start=True, stop=True), reads=[rk_r, rq_r], writes=[scp_r])
            sc, sc_r = scr.next()
            P.op("dve", lambda: nc.vector.tensor_tensor(out=sc[:], in0=scp[:].rearrange("p (h c) -> p h c", h=4),
                                                        in1=rt[:, 0:4, :], op=ALU.mult), reads=[scp_r, tab_r], writes=[sc_r])
            qx, qx_r = qxr.next()
            P.op("dve", lambda: nc.vector.tensor_tensor(out=qx[:], in0=rq[:], in1=rt[:, 4:8, :], op=ALU.mult),
                 reads=[rq_r, tab_r], writes=[qx_r])
            ypA, ypA_r = yA_pr.next()
            ypB, ypB_r = yB_pr.next()
            for h in range(4):
                yp, yp_r = (ypA, ypA_r) if h < 2 else (ypB, ypB_r)
                o = (h % 2) * 256
                P.op("pe", lambda: nc.tensor.matmul(yp[:, o:o + 256], lhsT=sc[:, h, :], rhs=rkv[:, 512 + h * 256:512 + (h + 1) * 256],
                                                    start=True, stop=False), reads=[sc_r, rkv_r], writes=[yp_r])
                P.op("pe", lambda: nc.tensor.matmul(yp[:, o:o + 256], lhsT=qx[:, h, :], rhs=RbS[:, j, h, :], start=False, stop=True),
                     reads=[qx_r, snap_r[j][h]], writes=[yp_r])
            c["ysb"], c["ysb_r"] = ysbr.next()
            P.op("act", lambda: nc.scalar.copy(out=c["ysb"][:, 0:512], in_=ypA[:]), reads=[ypA_r], writes=[c["ysb_r"]])
            P.op("act", lambda: nc.scalar.copy(out=c["ysb"][:, 512:1024], in_=ypB[:]), reads=[ypB_r], writes=[c["ysb_r"]])

        def stB(j):
            c = C[j]
            st, st_r = str_.next()
            c["mv"], c["mv_r"] = mvr.next()
            for h in range(4):
                P.op("dve", lambda: nc.vector.bn_stats(out=st[:, h, :], in_=c["ysb"][:, h * 256:(h + 1) * 256]),
                     reads=[c["ysb_r"]], writes=[st_r])
            for h in range(4):
                P.op("dve", lambda: nc.vector.bn_aggr(out=c["mv"][:, h, :], in_=st[:, h, :]), reads=[st_r], writes=[c["mv_r"]])

        def stC(j):
            c = C[j]
            ln_t, ln_r = lnr.next()
            P.op("act", lambda: nc.scalar.activation(out=ln_t[:], in_=c["mv"][:, :, 1], func=AF.Ln, bias=float(EPS), scale=1.0),
                 reads=[c["mv_r"]], writes=[ln_r])
            c["rs"], c["rs_r"] = rsr.next()
            P.op("act", lambda: nc.scalar.activation(out=c["rs"][:], in_=ln_t[:], func=AF.Exp, scale=-0.5),
                 reads=[ln_r], writes=[c["rs_r"]])

        def stD(j):
            c = C[j]
            c["nm"], c["nm_r"] = nmr_.next()
            P.op("dve", lambda: nc.vector.scalar_tensor_tensor(out=c["nm"][:], in0=c["mv"][:, :, 0], scalar=-1.0, in1=c["rs"][:],
                                                               op0=ALU.mult, op1=ALU.mult),
                 reads=[c["mv_r"], c["rs_r"]], writes=[c["nm_r"]])

        def stE(j):
            c = C[j]
            c["yn"], c["yn_r"] = ynr.next()
            for h in range(4):
                P.op("act", lambda: nc.scalar.activation(out=c["yn"][:, h * 256:(h + 1) * 256], in_=c["ysb"][:, h * 256:(h + 1) * 256],
                                                         func=AF.Identity, bias=c["nm"][:, h:h + 1], scale=c["rs"][:, h:h + 1]),
                     reads=[c["ysb_r"], c["nm_r"], c["rs_r"]], writes=[c["yn_r"]])

        def stF(j):
            c = C[j]
            c["yret"], c["yret_r"] = yretr.next()
            P.op("dve", lambda: nc.vector.tensor_tensor(out=c["yret"][:], in0=c["yn"][:], in1=c["gz"][:], op=ALU.mult),
                 reads=[c["yn_r"], c["gz_r"]], writes=[c["yret_r"]])

        def stG(j):
            c = C[j]
            c["tp"], c["tp_r"] = t_pr.next()
            for cc in range(8):
                P.op("pe", lambda: nc.tensor.transpose(out=c["tp"][:, cc * 128:(cc + 1) * 128],
                                                       in_=c["yret"][:, cc * 128:(cc + 1) * 128], identity=ident[:]),
                     reads=[c["yret_r"]], writes=[c["tp_r"]])

        def stH(j):
            c = C[j]
            P.op("act", lambda: nc.scalar.copy(out=YTS["t"][:, 0:8, j * 128:(j + 1) * 128],
                                               in_=c["tp"][:].rearrange("p (k t) -> p k t", k=8)),
                 reads=[c["tp_r"]], writes=[YTS["r"]])
            C[j] = None

        stages = [stA, stB, stC, stD, stE, stF, stG, stH]
        ns = len(stages)
        for step in range(NT_OWN + ns - 1):
            for k in reversed(range(ns)):
                j = step - k
                if 0 <= j < NT_OWN:
                    stages[k](j)
        P.barrier()
        stack.close()

    def phase_outproj(w2d, tiles, res_rows, dst_rows):
        stack = contextlib.ExitStack()
        actT = YTS["t"]
        act_rs = [YTS["r"]] * 16
        ws = WStream(stack, bw=512, nstg=1)
        ws.prefetch([(wsrc(w2d, 0, 512), 0, 512)])
        psr = Ring(nc, stack, "ps", [128, 512], F32, 4, psum=True)
        xrr = Ring(nc, stack, "xres", [128, 512], F32, 5)
        oor = Ring(nc, stack, "oo", [128, 512], F32, 4)
        blocks = [[(wsrc(w2d, c0, 512), 0, 512)] for c0 in range(0, D, 512)]

        work = [(bi, t) for bi in range(len(blocks)) for t in tiles]
        loaded = []

        def ensure(n):
            while len(loaded) < min(n, len(work)):
                bi, t = work[len(loaded)]
                xr_t, xr_r = xrr.next()
                P.dma("sp", xr_t[:], res_rows(t)[:, bi * 512:bi * 512 + 512], writes=[xr_r])
                loaded.append((xr_t, xr_r))

        def body(i, wb, wb_r):
            c0 = i * 512
            for tix, t in enumerate(tiles):
                if tix == len(tiles) // 2:
                    ws.mid()
                gidx = i * len(tiles) + tix
                ensure(gidx + 3)
                xr_t, xr_r = loaded[gidx]
                ps, ps_r = psr.next()
                for k in range(16):
                    P.op("pe", lambda: nc.tensor.matmul(ps[:], lhsT=actT[:, k, t * 128:(t + 1) * 128], rhs=wb[:, k, :],
                                                        start=(k == 0), stop=(k == 15)),
                         reads=[wb_r[k], act_rs[k]], writes=[ps_r])
                oo, oo_r = oor.next()
                P.op("dve", lambda: nc.vector.tensor_tensor(out=oo[:], in0=ps[:], in1=xr_t[:], op=ALU.add),
                     reads=[ps_r, xr_r], writes=[oo_r])
                P.dma("sp", dst_rows(t)[:, c0:c0 + 512], oo[:], reads=[oo_r])

        run_blocks(ws, blocks, body, pre=True)
        P.barrier()
        stack.close()

    def phase_ple(li, tiles, src_rows, dst_rows):
        stack = contextlib.ExitStack()
        actT = stack.enter_context(sbuf(nc, "actT", [128, 16, TOK_OWN], BF16))
        actT_r = Res("actT")
        pT = stack.enter_context(sbuf(nc, "pT", [128, 2, TOK_OWN], BF16))
        pT_r = Res("pT")
        ntl = len(tiles)
        wp = stack.enter_context(sbuf(nc, "wp", [128, 2, D], BF16))
        wp_r = Res("wp")
        P.dma("pool", wp[:], w_ple[li].rearrange("(k p) c -> p k c", p=128), writes=[wp_r])
        ws = WStream(stack, bw=512, nstg=1, gidx=1 + 2 * li)
        ws.prefetch([(wsrc(w_gate[li], 0, 512), 0, 512)])
        with contextlib.ExitStack() as s1:
            ppr = Ring(nc, s1, "pp", [128, 256], F32, 2)
            pbr_ = Ring(nc, s1, "ppb", [128, 256], BF16, 2)
            pps = Ring(nc, s1, "ppps", [128, 256], BF16, 1, psum=True)

            def extra(ti):
                t = tiles[ti]
                pp, pp_r = ppr.next()
                P.dma("sp", pp[:], pin[li, t * 128:(t + 1) * 128, :], writes=[pp_r])
                pb, pb_r = pbr_.next()
                P.op("pool", lambda: nc.gpsimd.tensor_copy(out=pb[:], in_=pp[:]), reads=[pp_r], writes=[pb_r])
                tp, tp_r = pps.next()
                for c in range(2):
                    P.op("pe", lambda: nc.tensor.transpose(out=tp[:, c * 128:(c + 1) * 128], in_=pb[:, c * 128:(c + 1) * 128],
                                                           identity=ident[:]), reads=[pb_r], writes=[tp_r])
                P.op("dve", lambda: nc.vector.tensor_copy(out=pT[:, :, ti * 128:(ti + 1) * 128],
                                                          in_=tp[:].rearrange("p (k t) -> p k t", k=2)),
                     reads=[tp_r], writes=[pT_r])

            norm_transpose(s1, lambda ti: src_rows(tiles[ti]), ntl, 1 + 2 * li, actT, actT_r, extra=extra, nx=4, nxs=3)
        P.barrier()
        psr = Ring(nc, stack, "ps", [128, 512], F32, 3, psum=True)
        ps2r = Ring(nc, stack, "ps2", [128, 512], F32, 3, psum=True)
        xrr = Ring(nc, stack, "xres", [128, 512], F32, 4)
        ggr = Ring(nc, stack, "gg", [128, 512], F32, 2)
        oor = Ring(nc, stack, "oo", [128, 512], F32, 3)
        blocks = [[(wsrc(w_gate[li], c0, 512), 0, 512)] for c0 in range(0, D, 512)]

        work = [(bi, t) for bi in range(len(blocks)) for t in tiles]
        loaded = []

        def ensure(n):
            while len(loaded) < min(n, len(work)):
                bi, t = work[len(loaded)]
                xr_t, xr_r = xrr.next()
                P.dma("sp", xr_t[:], src_rows(t)[:, bi * 512:bi * 512 + 512], writes=[xr_r])
                loaded.append((xr_t, xr_r))

        def body(i, wb, wb_r):
            c0 = i * 512
            for ti, t in enumerate(tiles):
                if ti == len(tiles) // 2:
                    ws.mid()
                gidx = i * len(tiles) + ti
                ensure(gidx + 3)
                xr_t, xr_r = loaded[gidx]
                ps, ps_r = psr.next()
                for k in range(16):
                    P.op("pe", lambda: nc.tensor.matmul(ps[:], lhsT=actT[:, k, ti * 128:(ti + 1) * 128], rhs=wb[:, k, :],
                                                        start=(k == 0), stop=(k == 15)),
                         reads=[wb_r[k], actT_r], writes=[ps_r])
                p2, p2_r = ps2r.next()
                for k in range(2):
                    P.op("pe", lambda: nc.tensor.matmul(p2[:], lhsT=pT[:, k, ti * 128:(ti + 1) * 128], rhs=wp[:, k, c0:c0 + 512],
                                                        start=(k == 0), stop=(k == 1)),
                         reads=[wp_r, pT_r], writes=[p2_r])
                gg, gg_r = ggr.next()
                P.op("act", lambda: nc.scalar.activation(out=gg[:], in_=ps[:], func=AF.Sigmoid), reads=[ps_r], writes=[gg_r])
                oo, oo_r = oor.next()
                P.op("dve", lambda: nc.vector.tensor_tensor(out=oo[:], in0=p2[:], in1=gg[:], op=ALU.mult),
                     reads=[p2_r, gg_r], writes=[oo_r])
                P.op("dve", lambda: nc.vector.tensor_tensor(out=oo[:], in0=oo[:], in1=xr_t[:], op=ALU.add),
                     reads=[oo_r, xr_r], writes=[oo_r])
                P.dma("sp", dst_rows(t)[:, c0:c0 + 512], oo[:], reads=[oo_r])

        run_blocks(ws, blocks, body, pre=True)
        P.barrier()
        stack.close()

    def phase_inproj1():
        stack = contextlib.ExitStack()
        actT = stack.enter_context(sbuf(nc, "actT", [128, 16, TOK_OWN], BF16))
        actT_r = Res("actT")
        ws = WStream(stack, gidx=2, nstg=1)
        wv = w_in1.rearrange("(k p) (j c) -> p k j c", p=128, j=4)
        blocks = []
        for cc in range(16):
            blocks.append([(wv[:, :, 1, cc * 128:(cc + 1) * 128], 0, 128), (wv[:, :, 2, cc * 128:(cc + 1) * 128], 128, 128)])
            blocks.append([(wv[:, :, 0, cc * 128:(cc + 1) * 128], 0, 128), (wv[:, :, 3, cc * 128:(cc + 1) * 128], 128, 128)])
        ws.prefetch(blocks[0])
        with contextlib.ExitStack() as s1:
            norm_transpose(s1, lambda t: X2[t * 128:(t + 1) * 128, :], NT_OWN, 2, actT, actT_r, nx=3, nxs=3)
        P.barrier()
        cw = stack.enter_context(sbuf(nc, "convw_s", [128, 16, 3], F32))
        cw_r = Res("cw")
        P.dma("sp", cw[:], convw[:, :, :], writes=[cw_r])
        cu = stack.enter_context(sbuf(nc, "cu", [128, TOK_OWN + 2], F32))
        cu_r = Res("cu")
        cv = stack.enter_context(sbuf(nc, "cv", [128, TOK_OWN], F32))
        cv_r = Res("cv")
        P.op("dve", lambda: nc.vector.memset(cu[:, 0:2], 0.0), writes=[cu_r])
        pa = Ring(nc, stack, "psa", [128, 512], F32, 4, psum=True)
        pb_ = Ring(nc, stack, "psb", [128, 512], F32, 4, psum=True)
        usr = Ring(nc, stack, "us", [128, 512], F32, 2)
        szr = Ring(nc, stack, "sz", [128, 512], F32, 2)
        t1r = Ring(nc, stack, "t1", [128, 512], F32, 2)
        groups = [(0, 128)] + [(128 + 512 * g, 512) for g in range(4)]
        def mm(ps, ps_r, wb, wb_r, half, g0, gn):
            for k in range(16):
                P.op("pe", lambda: nc.tensor.matmul(ps[:, 0:gn], lhsT=wb[:, k, half * 128:(half + 1) * 128],
                                                    rhs=actT[:, k, g0:g0 + gn], start=(k == 0), stop=(k == 15)),
                     reads=[wb_r[k], actT_r], writes=[ps_r])

        def body(i, wb, wb_r):
            cc = i // 2
            if i % 2 == 0:
                for gi_, (g0, gn) in enumerate(groups):
                    if gi_ == 3:
                        ws.mid()
                    p_c, p_c_r = pa.next()
                    mm(p_c, p_c_r, wb, wb_r, 0, g0, gn)
                    p_u, p_u_r = pb_.next()
                    mm(p_u, p_u_r, wb, wb_r, 1, g0, gn)
                    us, us_r = usr.next()
                    P.op("act", lambda: nc.scalar.copy(out=us[:, 0:gn], in_=p_u[:, 0:gn]), reads=[p_u_r], writes=[us_r])
                    P.op("dve", lambda: nc.vector.tensor_tensor(out=cu[:, 2 + g0:2 + g0 + gn], in0=p_c[:, 0:gn], in1=us[:, 0:gn],
                                                                op=ALU.mult), reads=[p_c_r, us_r], writes=[cu_r])
                P.op("act", lambda: nc.scalar.activation(out=cv[:], in_=cu[:, 0:TOK_OWN], func=AF.Copy, scale=cw[:, cc, 0:1]),
                     reads=[cu_r, cw_r], writes=[cv_r])
                P.op("dve", lambda: nc.vector.scalar_tensor_tensor(out=cv[:], in0=cu[:, 1:TOK_OWN + 1], scalar=cw[:, cc, 1:2], in1=cv[:],
                                                                   op0=ALU.mult, op1=ALU.add), reads=[cu_r, cw_r, cv_r], writes=[cv_r])
                P.op("dve", lambda: nc.vector.scalar_tensor_tensor(out=cv[:], in0=cu[:, 2:TOK_OWN + 2], scalar=cw[:, cc, 2:3], in1=cv[:],
                                                                   op0=ALU.mult, op1=ALU.add), reads=[cu_r, cw_r, cv_r], writes=[cv_r])
            else:
                for gi_, (g0, gn) in enumerate(groups):
                    if gi_ == 3:
                        ws.mid()
                    if gi_ == 0:
                        continue
                    p_b, p_b_r = pa.next()
                    mm(p_b, p_b_r, wb, wb_r, 0, g0, gn)
                    p_z, p_z_r = pb_.next()
                    mm(p_z, p_z_r, wb, wb_r, 1, g0, gn)
                    sz, sz_r = szr.next()
                    P.op("act", lambda: nc.scalar.activation(out=sz[:, 0:gn], in_=p_z[:, 0:gn], func=AF.Silu),
                         reads=[p_z_r], writes=[sz_r])
                    t1, t1_r = t1r.next()
                    P.op("dve", lambda: nc.vector.tensor_tensor(out=t1[:, 0:gn], in0=p_b[:, 0:gn], in1=cv[:, g0:g0 + gn], op=ALU.mult),
                         reads=[p_b_r, cv_r], writes=[t1_r])
                    P.op("dve", lambda: nc.vector.tensor_tensor(out=YTS["t"][:, cc, g0:g0 + gn], in0=t1[:, 0:gn], in1=sz[:, 0:gn],
                                                                op=ALU.mult), reads=[t1_r, sz_r], writes=[YTS["r"]])

        run_blocks(ws, blocks, body, pre=True)
        P.barrier()
        stack.close()

    own_rows_x = lambda t: xin[(NT_CTX + t) * 128:(NT_CTX + t + 1) * 128, :]
    rows = lambda X: (lambda t: X[t * 128:(t + 1) * 128, :])
    all_tiles = list(range(NT_OWN))
    own_tiles = list(range(1, NT_OWN))
    YTS = {}

    def yT_open():
        cm = sbuf(nc, "yT_res", [128, 16, TOK_OWN], BF16)
        YTS["cm"] = cm
        YTS["t"] = cm.__enter__()
        YTS["r"] = Res("yT")

    def yT_close():
        YTS["cm"].__exit__(None, None, None)

    ph = 0
    steps = [
        lambda: phase_inproj0(0, NT_CTX, False),
        lambda: (rs_open(), phase_inproj0(NT_CTX, NT_OWN, True)),
        lambda: (yT_open(), phase_attention()),
        phase_retention,
        lambda: (phase_outproj(w_out0, all_tiles, own_rows_x, rows(X1)), yT_close(), rs_close()),
        lambda: phase_ple(0, all_tiles, rows(X1), rows(X2)),
        lambda: (yT_open(), phase_inproj1()),
        lambda: (phase_outproj(w_out1, own_tiles, rows(X2), rows(X3)), yT_close()),
        lambda: phase_ple(1, own_tiles, rows(X3), lambda t: out[(t - 1) * 128:t * 128, :]),
    ]
    for i, st in enumerate(steps):
        if i < upto:
            st()
    P.barrier()
    gstack.close()
    P.close()
    return nc


def _tables():
    slopes = (2.0 ** (-8.0 / 8)) ** np.arange(1, 9)
    k = np.arange(128)[:, None].astype(np.float64)
    q = np.arange(128)[None, :].astype(np.float64)
    et = np.zeros((8, 128, ETW), np.float32)
    for h in range(8):
        for pi, d in enumerate(PATTERNS):
            rel = q - k
            diag = np.where(rel >= 0, np.exp(-slopes[h] * np.maximum(rel, 0.0) * d), 0.0)
            relp = q - k + 128
            prev = np.where(relp <= 128, np.exp(-slopes[h] * relp * d), 0.0)
            o = pi * 512
            et[h, :, o:o + 128] = prev
            et[h, :, o + 128:o + 256] = diag
            et[h, :, o + 256:o + 384] = prev
            et[h, :, o + 384:o + 512] = diag
            if d == 4:
                for r in range(4):
                    et[h, :, 1536 + r * 64:1536 + r * 64 + 32] = prev[:, 96:128]
                    et[h, :, 1536 + r * 64 + 32:1536 + r * 64 + 64] = diag[:, 96:128]
            if d == 16:
                for r in range(16):
                    et[h, :, 1792 + r * 8:1792 + (r + 1) * 8] = diag[:, 120:128]
    rt = np.zeros((128, 8, 128), np.float32)
    zt = np.zeros((128, 4), np.float32)
    for h in range(4):
        lg = np.log(RET_GAMMA[h])
        diff = q - k
        rt[:, h, :] = np.where(diff >= 0, np.exp(lg * np.maximum(diff, 0.0)), 0.0) * 128 ** -0.5
        rt[:, 4 + h, :] = np.exp(lg * (q + 1.0)) * np.ones((128, 1))
        zt[:, h] = np.exp(lg * (127.0 - np.arange(128))) * 128 ** -0.5
    return et, rt, zt


def make_in_maps(x, p, pre_norm_g, w_in_even, q_norm_g, k_norm_g, ret_gn_g, w_out_even,
                 w_in_odd, conv_w_odd, w_out_odd, ple_norm_g, w_ple_gate, w_ple_proj):
    f = lambda a: np.ascontiguousarray(np.asarray(a, dtype=np.float32))
    x, p = f(x), f(p)
    et, rt, zt = _tables()
    et0 = et.copy()
    for pi in range(3):
        et0[:, :, pi * 512 + 256:pi * 512 + 384] = 0.0
    gcol = np.stack([f(g).reshape(16, 128).T for g in
                     (pre_norm_g[0], ple_norm_g[0], pre_norm_g[1], ple_norm_g[1])]).astype(np.float32)
    qkg = np.stack([f(q_norm_g)[0], f(k_norm_g)[0]], axis=1)
    gng = np.broadcast_to(f(ret_gn_g)[0].reshape(1, 1024), (128, 1024))
    convw = f(conv_w_odd)[0].reshape(3, 16, 128).transpose(2, 1, 0)
    shared = {
        "w_in0": f(w_in_even)[0], "w_out0": f(w_out_even)[0], "w_in1": f(w_in_odd)[0], "w_out1": f(w_out_odd)[0],
        "w_gate": f(w_ple_gate), "w_ple": f(w_ple_proj), "gcol": np.ascontiguousarray(gcol), "qkg": np.ascontiguousarray(qkg),
        "gng": np.ascontiguousarray(gng), "convw": np.ascontiguousarray(convw), "rtab": rt, "zeta": zt,
        "ident": np.eye(128, dtype=np.float32).astype(ml_dtypes.bfloat16),
        "ones": np.ones((128, 128), np.float32).astype(ml_dtypes.bfloat16),
    }
    maps = []
    for c in range(8):
        b, s = c // 2, c % 2
        if s == 1:
            xin = x[b]
            pin = p[:, b, TOK_CTX:T, :]
            e = et
        else:
            xin = np.concatenate([np.zeros((2048, D), np.float32), x[b, 0:2048]], axis=0)
            pin = np.concatenate([np.zeros((2, 128, 256), np.float32), p[:, b, 0:2048, :]], axis=1)
            e = et0
        m = dict(shared)
        m["xin"] = np.ascontiguousarray(xin)
        m["pin"] = np.ascontiguousarray(pin)
        m["etab"] = e
        maps.append(m)
    return maps


def kernel(**inputs):
    maps = make_in_maps(**inputs)
    nc = build()
    res = run_bass_kernel_spmd(nc, maps, core_ids=list(range(8)))
    out = np.zeros((4, T, D), np.float32)
    for c in range(8):
        b, s = c // 2, c % 2
        out[b, s * 2048:(s + 1) * 2048, :] = res.results[c]["out"]
    return out
```

```python
import contextlib
import numpy as np
import ml_dtypes
import concourse.bass as bass
import concourse.mybir as mybir
from concourse.bass_utils import run_bass_kernel_spmd

F32 = mybir.dt.float32
BF16 = mybir.dt.bfloat16
AF = mybir.ActivationFunctionType
ALU = mybir.AluOpType

D = 2048
T = 4096
NT_ALL = 32
NT_CTX = 15
NT_OWN = 17
TOK_CTX = NT_CTX * 128
TOK_OWN = NT_OWN * 128
EPS = 1e-6
BW = 256
SAME_SYNC = True
ETW = 1920
PATTERNS = (1, 4, 16)
RET_GAMMA = [1.0 - 2.0 ** (-5.0 - h) for h in range(4)]


class Res:
    __slots__ = ("name", "w", "r")

    def __init__(self, name=""):
        self.name = name
        self.w = None
        self.r = {}


class Prog:
    def __init__(self, nc, ndma_sems=10):
        self.nc = nc
        self.eng = {"pe": nc.tensor, "dve": nc.vector, "act": nc.scalar, "pool": nc.gpsimd, "sp": nc.sync}
        self.sems = {}
        self.cnt = {}
        self._stack = []
        for e in self.eng:
            self.sems[e] = self._sem("s_" + e)
            self.cnt[e] = 0
        self.dring = {}
        for q in ("sp", "act", "pool"):
            ring = []
            for i in range(ndma_sems):
                k = "d_%s_%d" % (q, i)
                self.sems[k] = self._sem(k)
                self.cnt[k] = 0
                ring.append(k)
            self.dring[q] = [ring, 0]
        self.seen = {e: {} for e in self.eng}
        self.same_engine_sync = {"dve": SAME_SYNC, "act": SAME_SYNC, "pool": SAME_SYNC, "pe": False, "sp": False}

    def _sem(self, name):
        cm = self.nc.semaphore(name)
        h = cm.__enter__()
        self._stack.append(cm)
        return h

    def close(self):
        for cm in reversed(self._stack):
            cm.__exit__(None, None, None)

    def _wait(self, e, key, val):
        if val <= 0 or self.seen[e].get(key, 0) >= val:
            return
        self.eng[e].wait_ge(self.sems[key], val)
        self.seen[e][key] = val

    def _deps(self, e, reads, writes):
        for r in reads:
            if r.w is not None:
                k, v = r.w
                if k != e or self.same_engine_sync[e]:
                    self._wait(e, k, v)
        for w in writes:
            if w.w is not None:
                k, v = w.w
                if k != e or self.same_engine_sync[e]:
                    self._wait(e, k, v)
            for k, v in w.r.items():
                if k != e:
                    self._wait(e, k, v)

    def op(self, e, fn, reads=(), writes=()):
        self._deps(e, reads, writes)
        ins = fn()
        self.cnt[e] += 1
        ins.then_inc(self.sems[e], 1)
        v = self.cnt[e]
        for r in reads:
            if r.r.get(e, 0) < v:
                r.r[e] = v
        for w in writes:
            w.w = (e, v)
            w.r = {}
        return ins

    def dma(self, q, out, in_, reads=(), writes=(), **kw):
        ring, idx = self.dring[q]
        k = ring[idx % len(ring)]
        self.dring[q][1] = idx + 1
        self._wait(q, k, self.cnt[k])
        self._deps(q, reads, writes)
        ins = self.eng[q].dma_start(out=out, in_=in_, **kw)
        self.cnt[k] += 16
        ins.then_inc(self.sems[k], 16)
        v = self.cnt[k]
        for r in reads:
            r.r[k] = v
        for w in writes:
            w.w = (k, v)
            w.r = {}
        return ins

    def barrier(self):
        for e in self.eng:
            for k in self.sems:
                if k != e:
                    self._wait(e, k, self.cnt[k])


_UNIQ = [0]


def sbuf(nc, name, shape, dtype):
    _UNIQ[0] += 1
    return nc.sbuf_tensor("%s_%d" % (name, _UNIQ[0]), shape, dtype)


def psum(nc, name, shape, dtype):
    _UNIQ[0] += 1
    return nc.psum_tensor("%s_%d" % (name, _UNIQ[0]), shape, dtype)


class WRes:
    def __init__(self, lo, hi, hk):
        self.lo, self.hi, self.hk = lo, hi, hk

    def __getitem__(self, k):
        return self.lo if k < self.hk else self.hi


class Ring:
    def __init__(self, nc, stack, name, shape, dtype, n, psum=False):
        self.items = []
        for i in range(n):
            alloc = globals()["psum"] if psum else sbuf
            t = stack.enter_context(alloc(nc, "%s%d" % (name, i), shape, dtype))
            self.items.append((t, Res("%s%d" % (name, i))))
        self.i = 0

    def next(self):
        it = self.items[self.i % len(self.items)]
        self.i += 1
        return it


def att_items():
    items = []
    for pi, d in enumerate(PATTERNS):
        nb = T // d // 128
        for r in range(d):
            h0 = 1920 // d
            n = h0 // 128
            items.append((pi, d, r, n, h0 - n * 128, True))
            for n in range(2048 // d // 128, nb):
                items.append((pi, d, r, n, 0, False))
    return items


def build(dbg=False, upto=99):
    nc = bass.Bass("TRN2", target_bir_lowering=False)
    P = Prog(nc)

    def din(name, shape, dt=F32):
        return nc.dram_tensor(name, list(shape), dt, kind="ExternalInput").ap()

    def dscr(name, shape, dt=BF16):
        return nc.dram_tensor(name, list(shape), dt, kind="ExternalOutput" if dbg else "Internal").ap()

    xin = din("xin", [T, D])
    pin = din("pin", [2, TOK_OWN, 256])
    w_in0 = din("w_in0", [D, 7168])
    w_out0 = din("w_out0", [D, D])
    w_in1 = din("w_in1", [D, 8192])
    w_out1 = din("w_out1", [D, D])
    w_gate = din("w_gate", [2, D, D])
    w_ple = din("w_ple", [2, 256, D])
    gcol_d = din("gcol", [4, 128, 16])
    qkg = din("qkg", [128, 2])
    gng = din("gng", [128, 1024])
    convw = din("convw", [128, 16, 3])
    etab = din("etab", [8, 128, ETW])
    rtab = din("rtab", [128, 8, 128])
    zeta = din("zeta", [128, 4])
    ident_d = din("ident", [128, 128], BF16)
    ones_d = din("ones", [128, 128], BF16)
    out = nc.dram_tensor("out", [2048, D], F32, kind="ExternalOutput").ap()

    QT = dscr("QT", [8, 128, TOK_OWN])
    KT = dscr("KT", [8, 128, T])
    VT = dscr("VT", [8, 128, T])
    ZT = dscr("ZT", [8, 128, TOK_OWN])
    RQT = dscr("RQT", [4, 128, TOK_OWN])
    RKT = dscr("RKT", [4, 128, TOK_OWN])
    RKV = dscr("RKV", [NT_ALL, 128, 1536])
    ZR = dscr("ZR", [NT_OWN, 128, 1024])
    YT = dscr("YT", [16, 128, TOK_OWN])
    X1 = dscr("X1", [TOK_OWN, D], F32)
    X2 = dscr("X2", [TOK_OWN, D], F32)
    X3 = dscr("X3", [TOK_OWN, D], F32)

    gstack = contextlib.ExitStack()
    ident = gstack.enter_context(sbuf(nc, "ident_s", [128, 128], BF16))
    ones = gstack.enter_context(sbuf(nc, "ones_s", [128, 128], BF16))
    qkg_s = gstack.enter_context(sbuf(nc, "qkg_s", [128, 2], F32))
    qkg2 = gstack.enter_context(sbuf(nc, "qkg2", [128, 2], F32))
    r_const = Res("const")
    P.dma("sp", ident[:], ident_d[:, :], writes=[r_const])
    P.dma("sp", ones[:], ones_d[:, :], writes=[r_const])
    P.dma("sp", qkg_s[:], qkg[:, :], writes=[r_const])
    P.op("dve", lambda: nc.vector.tensor_scalar(out=qkg2[:, 0:1], in0=qkg_s[:, 0:1], scalar1=float(128 ** -0.5),
                                                 scalar2=None, op0=ALU.mult), reads=[r_const], writes=[r_const])
    P.op("dve", lambda: nc.vector.tensor_copy(out=qkg2[:, 1:2], in_=qkg_s[:, 1:2]), reads=[r_const], writes=[r_const])
    P.barrier()

    def rstd_from_sumsq(stack_pools, ss, ss_r, scale):
        (lnr, rsr) = stack_pools
        ln_t, ln_r = lnr.next()
        rs_t, rs_r = rsr.next()
        P.op("act", lambda: nc.scalar.activation(out=ln_t[:], in_=ss[:], func=AF.Ln, bias=float(EPS), scale=float(scale)),
             reads=[ss_r], writes=[ln_r])
        P.op("act", lambda: nc.scalar.activation(out=rs_t[:], in_=ln_t[:], func=AF.Exp, scale=-0.5),
             reads=[ln_r], writes=[rs_r])
        return rs_t, rs_r

    def norm_transpose(stack, src_rows, ntiles, gidx, actT, actT_r, extra=None, nx=4, nxs=4):
        xr = Ring(nc, stack, "xt", [128, D], F32, nx)
        xsr = Ring(nc, stack, "xs", [128, D], BF16, nxs)
        ssr = Ring(nc, stack, "ss", [128, 1], F32, 4)
        lnr = Ring(nc, stack, "lnv", [128, 1], F32, 4)
        rsr = Ring(nc, stack, "rsv", [128, 1], F32, 4)
        ptr = Ring(nc, stack, "ptr", [128, 1024], BF16, 2, psum=True)
        loaded_x = {}

        def load_x(t):
            if t < ntiles and t not in loaded_x:
                xt, xt_r = xr.next()
                P.dma("sp", xt[:], src_rows(t), writes=[xt_r])
                loaded_x[t] = (xt, xt_r)

        def stage1(t):
            for tt in range(t, t + nx - 1):
                load_x(tt)
            xt, xt_r = loaded_x.pop(t)
            xs, xs_r = xsr.next()
            ss, ss_r = ssr.next()
            P.op("act", lambda: nc.scalar.activation(out=xs[:], in_=xt[:], func=AF.Square, accum_out=ss[:]),
                 reads=[xt_r], writes=[xs_r, ss_r])
            rs, rs_r = rstd_from_sumsq((lnr, rsr), ss, ss_r, 1.0 / D)
            P.op("dve", lambda: nc.vector.tensor_scalar(out=xs[:], in0=xt[:], scalar1=rs[:, 0:1], scalar2=None, op0=ALU.mult),
                 reads=[xt_r, rs_r], writes=[xs_r])
            return xs, xs_r

        def stage2(t, xs, xs_r):
            for half in range(2):
                pt, pt_r = ptr.next()
                for j in range(8):
                    k = half * 8 + j
                    P.op("pe", lambda: nc.tensor.transpose(out=pt[:, j * 128:(j + 1) * 128],
                                                           in_=xs[:, k * 128:(k + 1) * 128], identity=ident[:]),
                         reads=[xs_r], writes=[pt_r])
                dst = actT[:, half * 8:(half + 1) * 8, t * 128:(t + 1) * 128]
                srcv = pt[:].rearrange("p (k t) -> p k t", k=8)
                if half == 0:
                    P.op("act", lambda: nc.scalar.copy(out=dst, in_=srcv), reads=[pt_r], writes=[actT_r])
                else:
                    P.op("dve", lambda: nc.vector.tensor_copy(out=dst, in_=srcv), reads=[pt_r], writes=[actT_r])
            if extra is not None:
                extra(t)

        pend = {0: stage1(0)}
        if ntiles > 1:
            pend[1] = stage1(1)
        for t in range(ntiles):
            if t + 2 < ntiles:
                pend[t + 2] = stage1(t + 2)
            xs, xs_r = pend.pop(t)
            stage2(t, xs, xs_r)

    class WStream:
        def __init__(self, stack, kc=16, bw=BW, nbuf=2, name="w", nstg=2, gidx=None):
            self.gc = None
            if gidx is not None:
                self.gc = stack.enter_context(sbuf(nc, "gcol_s", [128, 16], F32))
                self.gc_r = Res("gcol")
                P.dma("sp", self.gc[:], gcol_d[gidx, :, :], writes=[self.gc_r])
            self.kc = kc
            self.bw = bw
            self.stg = Ring(nc, stack, name + "stg", [128, kc, bw], F32, nstg)
            self.wb = Ring(nc, stack, name + "wb", [128, kc, bw], BF16, nbuf)
            self.wres = {id(r): (Res("wlo"), Res("whi")) for (_, r) in self.wb.items}
            self.pending = []

        def prefetch(self, srcs):
            st, st_r = self.stg.next()
            for ap, c0, n in srcs:
                P.dma("sp", st[:, :, c0:c0 + n], ap, writes=[st_r])
            self.pending.append((st, st_r))

        def mid(self):
            if getattr(self, "want", False) and self.nxt is None:
                self.nxt = self.get()

        def get(self):
            st, st_r = self.pending.pop(0)
            wb, wb_r0 = self.wb.next()
            hk = self.kc // 2
            r_lo, r_hi = self.wres[id(wb_r0)]
            if self.gc is None:
                P.op("dve", lambda: nc.vector.tensor_copy(out=wb[:, 0:hk, :], in_=st[:, 0:hk, :]), reads=[st_r], writes=[r_lo])
                P.op("act", lambda: nc.scalar.copy(out=wb[:, hk:, :], in_=st[:, hk:, :]), reads=[st_r], writes=[r_hi])
            else:
                gc, gc_r = self.gc, self.gc_r
                for k in range(hk):
                    P.op("dve", lambda: nc.vector.tensor_scalar(out=wb[:, k, :], in0=st[:, k, :], scalar1=gc[:, k:k + 1],
                                                                scalar2=None, op0=ALU.mult), reads=[st_r, gc_r], writes=[r_lo])
                    k2 = hk + k
                    P.op("act", lambda: nc.scalar.activation(out=wb[:, k2, :], in_=st[:, k2, :], func=AF.Copy,
                                                             scale=gc[:, k2:k2 + 1]), reads=[st_r, gc_r], writes=[r_hi])
            return wb, WRes(r_lo, r_hi, hk)

    def wsrc(w2d, c0, n, kc=16):
        return w2d.rearrange("(k p) c -> p k c", p=128)[:, :, c0:c0 + n]

    def run_blocks(ws, blocks, body, pre=False):
        if not blocks:
            return
        if not pre:
            ws.prefetch(blocks[0])
        cur = ws.get()
        for i in range(len(blocks)):
            has_next = i + 1 < len(blocks)
            if has_next:
                ws.prefetch(blocks[i + 1])
            ws.nxt = None
            ws.want = has_next
            body(i, cur[0], cur[1])
            ws.mid()
            cur = ws.nxt

    RS = {}
    RKVD = [[Res("rkvd") for _ in range(3)] for _ in range(NT_ALL)]

    def rs_open():
        RS["stack"] = st = contextlib.ExitStack()
        RS["zt"] = st.enter_context(sbuf(nc, "zeta_s", [128, 4], F32))
        RS["zt_r"] = Res("zeta")
        P.dma("sp", RS["zt"][:], zeta[:, :], writes=[RS["zt_r"]])
        RS["R"] = st.enter_context(sbuf(nc, "Rst", [128, 4, 256], F32))
        RS["RbS"] = st.enter_context(sbuf(nc, "RbS", [128, NT_OWN, 4, 256], BF16))
        RS["R_r"] = [Res("R%d" % h) for h in range(4)]
        RS["snap_r"] = [[Res("snap") for h in range(4)] for j in range(NT_OWN)]
        for h in range(4):
            P.op("dve", lambda: nc.vector.memset(RS["R"][:, h, :], 0.0), writes=[RS["R_r"][h]])
        RS["next_t"] = 0
        RS["queue"] = []

    def rs_close():
        RS["stack"].close()

    def pass1_stageA(n, rkvr, kzr):
        zt, zt_r = RS["zt"], RS["zt_r"]
        for _ in range(n):
            t = RS["next_t"]
            if t >= NT_ALL - 1:
                return
            RS["next_t"] = t + 1
            rkv, rkv_r = rkvr.next()
            P.dma("sp", rkv[:], RKV[t, :, :], reads=(RKVD[t] if t >= NT_CTX else []), writes=[rkv_r])
            kzs = []
            for h in range(4):
                kz, kz_r = kzr.next()
                P.op("pool", lambda: nc.gpsimd.tensor_scalar(out=kz[:], in0=rkv[:, h * 128:(h + 1) * 128], scalar1=zt[:, h:h + 1],
                                                             scalar2=None, op0=ALU.mult), reads=[rkv_r, zt_r], writes=[kz_r])
                kzs.append((kz, kz_r))
            RS["queue"].append((t, rkv, rkv_r, kzs))

    def pass1_stageB(S_pr):
        R, RbS, R_r, snap_r = RS["R"], RS["RbS"], RS["R_r"], RS["snap_r"]
        while RS["queue"]:
            t, rkv, rkv_r, kzs = RS["queue"].pop(0)
            j = t - NT_CTX
            for h in range(4):
                if j >= 0:
                    P.op("pool", lambda: nc.gpsimd.tensor_copy(out=RbS[:, j, h, :], in_=R[:, h, :]), reads=[R_r[h]],
                         writes=[snap_r[j][h]])
                kz, kz_r = kzs[h]
                Sp, Sp_r = S_pr.next()
                P.op("pe", lambda: nc.tensor.matmul(Sp[:], lhsT=kz[:], rhs=rkv[:, 512 + h * 256:512 + (h + 1) * 256],
                                                    start=True, stop=True), reads=[kz_r, rkv_r], writes=[Sp_r])
                gch = float(RET_GAMMA[h] ** 128)
                P.op("dve", lambda: nc.vector.scalar_tensor_tensor(out=R[:, h, :], in0=R[:, h, :], scalar=gch, in1=Sp[:],
                                                                   op0=ALU.mult, op1=ALU.add),
                     reads=[Sp_r, R_r[h]], writes=[R_r[h]])
            if t == NT_ALL - 2:
                for h in range(4):
                    jj = NT_OWN - 1
                    P.op("pool", lambda: nc.gpsimd.tensor_copy(out=RbS[:, jj, h, :], in_=R[:, h, :]), reads=[R_r[h]],
                         writes=[snap_r[jj][h]])

    def phase_inproj0(tile0, ntiles, own):
        ntok = ntiles * 128
        stack = contextlib.ExitStack()
        actT = stack.enter_context(sbuf(nc, "actT", [128, 16, TOK_OWN], BF16))
        actT_r = Res("actT")
        ws = WStream(stack, bw=512, nstg=1, gidx=0)
        first_col = 512 if own else 3072
        ws.prefetch([(wsrc(w_in0, first_col, 512), 0, 512)])
        with contextlib.ExitStack() as s1:
            norm_transpose(s1, lambda t: xin[(tile0 + t) * 128:(tile0 + t + 1) * 128, :], ntiles, 0, actT, actT_r,
                           nx=(3 if own else 4), nxs=(3 if own else 4))
        P.barrier()
        if own:
            p1_rkvr = Ring(nc, stack, "p1rkv", [128, 1536], BF16, 6)
            p1_kzr = Ring(nc, stack, "p1kz", [128, 128], BF16, 24)
            p1_S = Ring(nc, stack, "p1S", [128, 256], F32, 2, psum=True)
        psr = Ring(nc, stack, "ps", [128, 512], F32, 4, psum=True)
        ps2r = Ring(nc, stack, "ps2", [128, 512], F32, 2, psum=True)
        stgr = Ring(nc, stack, "ostg", [128, 512], BF16, 4)
        sqr = Ring(nc, stack, "sq", [128, 512], BF16, 2)
        lnr = Ring(nc, stack, "lnw", [128, 512], F32, 1)
        rsr = Ring(nc, stack, "rsw", [128, 512], F32, 2)
        if own:
            groups = [(0, 128)] + [(128 + 512 * g, 512) for g in range(4)]
        else:
            groups = [(0, 512), (512, 512), (1024, 512), (1536, 384)]
        tok_off_all = tile0 * 128

        fm = []
        if own:
            for h in range(4):
                fm.append((0 + h * 128, "copy", RQT, h, 0, None))
            for h in range(4):
                fm.append((512 + h * 128, "copy", RKT, h, 0, None))
            for h in range(8):
                fm.append((2048 + h * 128, "qk", QT, h, 0, 0))
        for h in range(8):
            fm.append((3072 + h * 128, "qk", KT, h, tok_off_all, 1))
        for h in range(8):
            fm.append((4096 + h * 128, "copy", VT, h, tok_off_all, None))
        if own:
            for h in range(8):
                fm.append((6144 + h * 128, "silu", ZT, h, 0, None))
        fm_blocks = [fm[i:i + 4] for i in range(0, len(fm), 4)]

        deferred = []

        def flush():
            while deferred:
                deferred.pop(0)()

        def fm_body(i, wb, wb_r):
            for ci, (col0, kind, dst, di, toff, gi) in enumerate(fm_blocks[i]):
                for (g0, gn) in groups:
                    fm_group(wb, wb_r, ci, kind, dst, di, toff, gi, g0, gn)
                if ci == 1:
                    ws.mid()
            flush()

        def fm_group(wb, wb_r, ci, kind, dst, di, toff, gi, g0, gn):
                    ps, ps_r = psr.next()
                    for k in range(16):
                        P.op("pe", lambda: nc.tensor.matmul(ps[:, 0:gn], lhsT=wb[:, k, ci * 128:(ci + 1) * 128],
                                                            rhs=actT[:, k, g0:g0 + gn], start=(k == 0), stop=(k == 15)),
                             reads=[wb_r[k], actT_r], writes=[ps_r])
                    og, og_r = stgr.next()
                    if kind == "copy":
                        P.op("act", lambda: nc.scalar.copy(out=og[:, 0:gn], in_=ps[:, 0:gn]), reads=[ps_r], writes=[og_r])
                    elif kind == "silu":
                        P.op("act", lambda: nc.scalar.activation(out=og[:, 0:gn], in_=ps[:, 0:gn], func=AF.Silu),
                             reads=[ps_r], writes=[og_r])
                    else:
                        sq, sq_r = sqr.next()
                        P.op("act", lambda: nc.scalar.activation(out=sq[:, 0:gn], in_=ps[:, 0:gn], func=AF.Square),
                             reads=[ps_r], writes=[sq_r])
                        flush()
                        deferred.append(lambda: qk_tail(ps, ps_r, sq, sq_r, og, og_r, gn, gi, dst, di, toff, g0))
                        return
                    flush()
                    P.dma("sp", dst[di, :, toff + g0:toff + g0 + gn], og[:, 0:gn], reads=[og_r])

        def qk_tail(ps, ps_r, sq, sq_r, og, og_r, gn, gi, dst, di, toff, g0):
                        p2, p2_r = ps2r.next()
                        P.op("pe", lambda: nc.tensor.matmul(p2[:, 0:gn], lhsT=ones[:], rhs=sq[:, 0:gn], start=True, stop=True),
                             reads=[sq_r, r_const], writes=[p2_r])
                        ln_t, ln_r = lnr.next()
                        P.op("act", lambda: nc.scalar.activation(out=ln_t[:, 0:gn], in_=p2[:, 0:gn], func=AF.Ln,
                                                                 bias=float(EPS), scale=1.0 / 128),
                             reads=[p2_r], writes=[ln_r])
                        rs_t, rs_r = rsr.next()
                        P.op("act", lambda: nc.scalar.activation(out=rs_t[:, 0:gn], in_=ln_t[:, 0:gn], func=AF.Exp, scale=-0.5),
                             reads=[ln_r], writes=[rs_r])
                        P.op("dve", lambda: nc.vector.scalar_tensor_tensor(out=og[:, 0:gn], in0=ps[:, 0:gn],
                                                                           scalar=qkg2[:, gi:gi + 1], in1=rs_t[:, 0:gn],
                                                                           op0=ALU.mult, op1=ALU.mult),
                             reads=[ps_r, rs_r, r_const], writes=[og_r])
                        P.dma("sp", dst[di, :, toff + g0:toff + g0 + gn], og[:, 0:gn], reads=[og_r])

        tm = [(512 + j * 512, "copy", j * 512) for j in range(3)]
        if own:
            tm += [(5120 + j * 512, "silu", j * 512) for j in range(2)]

        def tm_body(i, wb, wb_r):
            col0, kind, dcol = tm[i]
            for t in range(ntiles):
                if t == ntiles // 2:
                    ws.mid()
                ps, ps_r = psr.next()
                for k in range(16):
                    P.op("pe", lambda: nc.tensor.matmul(ps[:], lhsT=actT[:, k, t * 128:(t + 1) * 128], rhs=wb[:, k, :],
                                                        start=(k == 0), stop=(k == 15)),
                         reads=[wb_r[k], actT_r], writes=[ps_r])
                og, og_r = stgr.next()
                fn = AF.Copy if kind == "copy" else AF.Silu
                P.op("act", lambda: nc.scalar.activation(out=og[:], in_=ps[:], func=fn), reads=[ps_r], writes=[og_r])
                if kind == "copy":
                    dstap = RKV[tile0 + t, :, dcol:dcol + 512]
                    wr = [RKVD[tile0 + t][i]]
                else:
                    dstap = ZR[t, :, dcol:dcol + 512]
                    wr = []
                P.dma("sp", dstap, og[:], reads=[og_r], writes=wr)

        nfm = len(fm_blocks)
        fm_src = [[(wsrc(w_in0, blk[0][0], 512), 0, 512)] for blk in fm_blocks]
        tm_src = [[(wsrc(w_in0, c0, 512), 0, 512)] for (c0, _, _) in tm]
        if own:
            order = [("tm", 0), ("tm", 1), ("tm", 2)] + [("fm", b) for b in range(nfm)] + [("tm", 3), ("tm", 4)]
        else:
            order = [("fm", b) for b in range(nfm)] + [("tm", b) for b in range(len(tm))]
        all_blocks = [(tm_src[b] if kind == "tm" else fm_src[b]) for (kind, b) in order]

        def body_all(i, wb, wb_r):
            kind, b = order[i]
            if kind == "tm":
                tm_body(b, wb, wb_r)
            else:
                fm_body(b, wb, wb_r)
            if own:
                pass1_stageB(p1_S)
                if i >= 3 or RS["next_t"] + 3 <= NT_CTX:
                    pass1_stageA(3 if i < 5 else 2, p1_rkvr, p1_kzr)

        run_blocks(ws, all_blocks, body_all, pre=True)
        if own:
            while RS["next_t"] < NT_ALL - 1 or RS["queue"]:
                pass1_stageB(p1_S)
                pass1_stageA(2, p1_rkvr, p1_kzr)
        P.barrier()
        stack.close()

    def phase_attention():
        stack = contextlib.ExitStack()
        qr = Ring(nc, stack, "qT", [128, TOK_OWN], BF16, 2)
        kr = Ring(nc, stack, "kT", [128, T], BF16, 2)
        vr = Ring(nc, stack, "vT", [128, T], BF16, 2)
        zr = Ring(nc, stack, "zT", [128, TOK_OWN], BF16, 2)
        er = Ring(nc, stack, "et", [128, ETW], F32, 2)
        vkeys = []
        for pi, d in enumerate(PATTERNS):
            nb = T // d // 128
            h0 = 1920 // d
            nh = h0 // 128
            n_first = 2048 // d // 128
            for r in range(d):
                blks = set()
                for n in [nh] + list(range(n_first, nb)):
                    blks.add(n)
                    if n >= 1:
                        blks.add(n - 1)
                for kb in sorted(blks):
                    vkeys.append((pi, r, kb))
        vslot = {k: i for i, k in enumerate(vkeys)}
        nvb = len(vkeys)
        vtok = stack.enter_context(sbuf(nc, "vtok", [128, nvb, 128], BF16))
        vtok_r = Res("vtok")
        p16 = stack.enter_context(sbuf(nc, "p16", [128, 16, 256], BF16))
        p16_r = Res("p16")
        yT, yT_r = YTS["t"], YTS["r"]
        ptr = Ring(nc, stack, "aptr", [128, 1024], BF16, 1, psum=True)
        spr = Ring(nc, stack, "sps", [128, 512], F32, 3, psum=True)
        obr = Ring(nc, stack, "obank", [128, 512], F32, 2, psum=True)
        dbr = Ring(nc, stack, "dbank", [128, 512], F32, 2, psum=True)
        pxr = Ring(nc, stack, "pexp", [128, 256], BF16, 3)
        pbr = Ring(nc, stack, "pT", [128, 256], BF16, 4)
        lnr = Ring(nc, stack, "lnd", [128, 512], F32, 1)
        rdr = Ring(nc, stack, "rd", [128, 512], F32, 1)
        y1r = Ring(nc, stack, "y1", [128, 512], F32, 1)

        def blk_slice(d, r, n, c0=0, off=0, cnt=None):
            start = r + d * (128 * n + c0) - off
            if cnt is None:
                cnt = 128 - c0
            return slice(start, start + d * (cnt - 1) + 1, d)

        def head_loads(hh):
            qT, q_r = qr.next()
            kT, k_r = kr.next()
            vT, v_r = vr.next()
            zT, z_r = zr.next()
            et, e_r = er.next()
            P.dma("sp", vT[:], VT[hh, :, :], writes=[v_r])
            P.dma("sp", kT[:], KT[hh, :, :], writes=[k_r])
            P.dma("sp", qT[:], QT[hh, :, :], writes=[q_r])
            P.dma("sp", et[:], etab[hh, :, :], writes=[e_r])
            P.dma("sp", zT[:], ZT[hh, :, :], writes=[z_r])
            return (qT, q_r, kT, k_r, vT, v_r, zT, z_r, et, e_r)

        hl = {0: head_loads(0)}
        for hh in range(8):
            (qT, q_r, kT, k_r, vT, v_r, zT, z_r, et, e_r) = hl.pop(hh)
            if hh + 1 < 8:
                hl[hh + 1] = head_loads(hh + 1)
            for b0 in range(0, nvb, 8):
                nb_ = min(8, nvb - b0)
                pt, pt_r = ptr.next()
                for j in range(nb_):
                    pi, r, kb = vkeys[b0 + j]
                    d = PATTERNS[pi]
                    P.op("pe", lambda: nc.tensor.transpose(out=pt[:, j * 128:(j + 1) * 128],
                                                           in_=vT[:, blk_slice(d, r, kb)], identity=ident[:]),
                         reads=[v_r], writes=[pt_r])
                eng = "act" if (b0 // 8) % 2 == 0 else "dve"
                dst = vtok[:, b0:b0 + nb_, :]
                srcv = pt[:, 0:nb_ * 128].rearrange("p (k t) -> p k t", k=nb_)
                if eng == "act":
                    P.op("act", lambda: nc.scalar.copy(out=dst, in_=srcv), reads=[pt_r], writes=[vtok_r])
                else:
                    P.op("dve", lambda: nc.vector.tensor_copy(out=dst, in_=srcv), reads=[pt_r], writes=[vtok_r])

            def make_p(pi, d, r, n, ctx, dst, dst_r):
                qs = blk_slice(d, r, n, 0, off=TOK_CTX)
                sp, sp_r = spr.next()
                for si, kb in enumerate((n - 1, n)):
                    P.op("pe", lambda: nc.tensor.matmul(sp[:, si * 128:(si + 1) * 128], lhsT=kT[:, blk_slice(d, r, kb)],
                                                        rhs=qT[:, qs], start=True, stop=True),
                         reads=[k_r, q_r], writes=[sp_r])
                px, px_r = pxr.next()
                P.op("act", lambda: nc.scalar.activation(out=px[:], in_=sp[:, 0:256], func=AF.Exp), reads=[sp_r], writes=[px_r])
                t0 = pi * 512 + (256 if ctx else 0)
                P.op("dve", lambda: nc.vector.tensor_tensor(out=dst, in0=px[:], in1=et[:, t0:t0 + 256], op=ALU.mult),
                     reads=[px_r, e_r], writes=[dst_r])

            for r in range(16):
                make_p(2, 16, r, 1, True, p16[:, r, :], p16_r)

            state = {}

            def acc(ob, ob_r, db, db_r, out_sl, lhs_slot, rhs, rhs_r):
                first = state["first"]
                P.op("pe", lambda: nc.tensor.matmul(ob[:, out_sl], lhsT=vtok[:, lhs_slot, :], rhs=rhs, start=first, stop=False,
                                                    skip_group_check=True),
                     reads=[vtok_r, rhs_r], writes=[ob_r])
                P.op("pe", lambda: nc.tensor.matmul(db[:, out_sl], lhsT=ones[:], rhs=rhs, start=first, stop=False,
                                                    skip_group_check=True),
                     reads=[rhs_r, r_const], writes=[db_r])
                state["first"] = False

            seq = []
            cur = {}

            def begin_group():
                cur["ob"], cur["ob_r"] = obr.next()
                cur["db"], cur["db_r"] = dbr.next()
                state["first"] = True

            def acc2(out_sl, lhs_slot, rhs, rhs_r):
                acc(cur["ob"], cur["ob_r"], cur["db"], cur["db_r"], out_sl, lhs_slot, rhs, rhs_r)

            def end_group(tok0, gw):
                ob, ob_r, db, db_r = cur["ob"], cur["ob_r"], cur["db"], cur["db_r"]
                ln_t, ln_r = lnr.next()
                P.op("act", lambda: nc.scalar.activation(out=ln_t[:, 0:gw], in_=db[:, 0:gw], func=AF.Ln), reads=[db_r], writes=[ln_r])
                rd, rd_r = rdr.next()
                P.op("act", lambda: nc.scalar.activation(out=rd[:, 0:gw], in_=ln_t[:, 0:gw], func=AF.Exp, scale=-1.0),
                     reads=[ln_r], writes=[rd_r])
                y1, y1_r = y1r.next()
                P.op("dve", lambda: nc.vector.tensor_tensor(out=y1[:, 0:gw], in0=ob[:, 0:gw], in1=rd[:, 0:gw], op=ALU.mult),
                     reads=[ob_r, rd_r], writes=[y1_r])
                P.op("dve", lambda: nc.vector.tensor_tensor(out=yT[:, 8 + hh, tok0:tok0 + gw], in0=y1[:, 0:gw],
                                                            in1=zT[:, tok0:tok0 + gw], op=ALU.mult),
                     reads=[y1_r, z_r], writes=[yT_r])

            def std_job(pi, d, r, n, ctx, out_sl):
                def pfn():
                    pb, pb_r = pbr.next()
                    make_p(pi, d, r, n, ctx, pb[:], pb_r)
                    return pb, pb_r

                def afn(pb, pb_r):
                    for si, kb in enumerate((n - 1, n)):
                        acc2(out_sl, vslot[(pi, r, kb)], pb[:, si * 128:(si + 1) * 128], pb_r)
                return ("job", pfn, afn)

            def halo4_p():
                sp, sp_r = spr.next()
                for r in range(4):
                    qs = blk_slice(4, r, 3, 96, off=TOK_CTX)
                    for si, kb in enumerate((2, 3)):
                        c = (r * 2 + si) * 32
                        P.op("pe", lambda: nc.tensor.matmul(sp[:, c:c + 32], lhsT=kT[:, blk_slice(4, r, kb)], rhs=qT[:, qs],
                                                            start=True, stop=True), reads=[k_r, q_r], writes=[sp_r])
                px, px_r = pxr.next()
                P.op("act", lambda: nc.scalar.activation(out=px[:], in_=sp[:, 0:256], func=AF.Exp), reads=[sp_r], writes=[px_r])
                pb, pb_r = pbr.next()
                P.op("dve", lambda: nc.vector.tensor_tensor(out=pb[:], in0=px[:], in1=et[:, 1536:1792], op=ALU.mult),
                     reads=[px_r, e_r], writes=[pb_r])
                return pb, pb_r

            def halo4_a(pb, pb_r):
                for r in range(4):
                    for si, kb in enumerate((2, 3)):
                        c = (r * 2 + si) * 32
                        acc2(slice(r, r + 4 * 31 + 1, 4), vslot[(1, r, kb)], pb[:, c:c + 32], pb_r)

            def halo16_p():
                sp, sp_r = spr.next()
                for r in range(16):
                    qs = blk_slice(16, r, 0, 120, off=TOK_CTX)
                    P.op("pe", lambda: nc.tensor.matmul(sp[:, r * 8:(r + 1) * 8], lhsT=kT[:, blk_slice(16, r, 0)], rhs=qT[:, qs],
                                                        start=True, stop=True), reads=[k_r, q_r], writes=[sp_r])
                px, px_r = pxr.next()
                P.op("act", lambda: nc.scalar.activation(out=px[:, 0:128], in_=sp[:, 0:128], func=AF.Exp),
                     reads=[sp_r], writes=[px_r])
                pb, pb_r = pbr.next()
                P.op("dve", lambda: nc.vector.tensor_tensor(out=pb[:, 0:128], in0=px[:, 0:128], in1=et[:, 1792:1920], op=ALU.mult),
                     reads=[px_r, e_r], writes=[pb_r])
                return pb, pb_r

            def halo16_a(pb, pb_r):
                for r in range(16):
                    acc2(slice(r, r + 16 * 7 + 1, 16), vslot[(2, r, 0)], pb[:, r * 8:(r + 1) * 8], pb_r)

            def d16_accs(g):
                def fn():
                    for r in range(16):
                        for si, kb in enumerate((0, 1)):
                            c = si * 128 + 32 * (g - 1)
                            acc2(slice(r, r + 16 * 31 + 1, 16), vslot[(2, r, kb)], p16[:, r, c:c + 32], p16_r)
                return fn

            for g in range(5):
                seq.append(("call", begin_group))
                if g == 0:
                    seq.append(std_job(0, 1, 0, 15, False, slice(0, 128)))
                    seq.append(("job", halo4_p, halo4_a))
                    seq.append(("job", halo16_p, halo16_a))
                    seq.append(("call", lambda: end_group(0, 128)))
                else:
                    seq.append(("call", d16_accs(g)))
                    for j in range(4):
                        n = 16 + 4 * (g - 1) + j
                        seq.append(std_job(0, 1, 0, n, n == 16, slice(j * 128, (j + 1) * 128)))
                    for r in range(4):
                        n = 4 + (g - 1)
                        seq.append(std_job(1, 4, r, n, n == 4, slice(r, r + 4 * 127 + 1, 4)))
                    seq.append(("call", (lambda g=g: end_group(128 + 512 * (g - 1), 512))))
            jobs = [e for e in seq if e[0] == "job"]
            LOOK = 2
            results = {}
            for i in range(min(LOOK, len(jobs))):
                results[i] = jobs[i][1]()
            ji = 0
            for e in seq:
                if e[0] == "call":
                    e[1]()
                else:
                    if ji + LOOK < len(jobs):
                        results[ji + LOOK] = jobs[ji + LOOK][1]()
                    pb, pb_r = results.pop(ji)
                    e[2](pb, pb_r)
                    ji += 1
        P.barrier()
        stack.close()

    def phase_retention():
        stack = contextlib.ExitStack()
        rt = stack.enter_context(sbuf(nc, "rtab_s", [128, 8, 128], F32))
        gn = stack.enter_context(sbuf(nc, "gng_s", [128, 1024], F32))
        tab_r = Res("rtabs")
        P.dma("sp", rt[:], rtab[:, :, :], writes=[tab_r])
        P.dma("sp", gn[:], gng[:, :], writes=[tab_r])
        RbS, snap_r = RS["RbS"], RS["snap_r"]
        rkvr = Ring(nc, stack, "rkv", [128, 1536], BF16, 3)
        rqr = Ring(nc, stack, "rq", [128, 4, 128], BF16, 3)
        rkr = Ring(nc, stack, "rk", [128, 4, 128], BF16, 3)
        zrr = Ring(nc, stack, "zr", [128, 1024], BF16, 3)
        gzr = Ring(nc, stack, "gz", [128, 1024], F32, 6)
        scr = Ring(nc, stack, "scT", [128, 4, 128], BF16, 3)
        qxr = Ring(nc, stack, "qxi", [128, 4, 128], BF16, 3)
        ysbr = Ring(nc, stack, "ysb", [128, 1024], F32, 6)
        ynr = Ring(nc, stack, "yn", [128, 1024], F32, 2)
        yretr = Ring(nc, stack, "yret", [128, 1024], BF16, 2)
        str_ = Ring(nc, stack, "bst", [128, 4, 6], F32, 3)
        mvr = Ring(nc, stack, "bmv", [128, 4, 2], F32, 6)
        lnr = Ring(nc, stack, "rln", [128, 4], F32, 4)
        rsr = Ring(nc, stack, "rrs", [128, 4], F32, 5)
        nmr_ = Ring(nc, stack, "nmr", [128, 4], F32, 4)
        sc_pr = Ring(nc, stack, "scps", [128, 512], F32, 2, psum=True)
        yA_pr = Ring(nc, stack, "ypsA", [128, 512], F32, 2, psum=True)
        yB_pr = Ring(nc, stack, "ypsB", [128, 512], F32, 2, psum=True)
        t_pr = Ring(nc, stack, "rtp", [128, 1024], BF16, 2, psum=True)
        C = [dict() for _ in range(NT_OWN)]

        def stA(j):
            c = C[j]
            rkv, rkv_r = rkvr.next()
            P.dma("sp", rkv[:], RKV[NT_CTX + j, :, :], writes=[rkv_r])
            rq, rq_r = rqr.next()
            P.dma("sp", rq[:], RQT[:, :, j * 128:(j + 1) * 128].rearrange("h p t -> p h t"), writes=[rq_r])
            rk, rk_r = rkr.next()
            P.dma("sp", rk[:], RKT[:, :, j * 128:(j + 1) * 128].rearrange("h p t -> p h t"), writes=[rk_r])
            zr_t, zr_r = zrr.next()
            P.dma("sp", zr_t[:], ZR[j, :, :], writes=[zr_r])
            c["gz"], c["gz_r"] = gzr.next()
            P.op("pool", lambda: nc.gpsimd.tensor_tensor(out=c["gz"][:], in0=zr_t[:], in1=gn[:], op=ALU.mult),
                 reads=[zr_r, tab_r], writes=[c["gz_r"]])
            scp, scp_r = sc_pr.next()
            for h in range(4):
                P.op("pe", lambda: nc.tensor.matmul(scp[:, h * 128:(h + 1) * 128], lhsT=rk[:, h, :], rhs=rq[:, h, :],
                                                    start=True, stop=True), reads=[rk_r, rq_r], writes=[scp_r])
            sc, sc_r = scr.next()
            P.op("dve", lambda: nc.vector.tensor_tensor(out=sc[:], in0=scp[:].rearrange("p (h c) -> p h c", h=4),
                                                        in1=rt[:, 0:4, :], op=ALU.mult), reads=[scp_r, tab_r], writes=[sc_r])
            qx, qx_r = qxr.next()
            P.op("dve", lambda: nc.vector.tensor_tensor(out=qx[:], in0=rq[:], in1=rt[:, 4:8, :], op=ALU.mult),
                 reads=[rq_r, tab_r], writes=[qx_r])
            ypA, ypA_r = yA_pr.next()
            ypB, ypB_r = yB_pr.next()
            for h in range(4):
                yp, yp_r = (ypA, ypA_r) if h < 2 else (ypB, ypB_r)
                o = (h % 2) * 256
                P.op("pe", lambda: nc.tensor.matmul(yp[:, o:o + 256], lhsT=sc[:, h, :], rhs=rkv[:, 512 + h * 256:512 + (h + 1) * 256],
                                                    start=True, stop=False), reads=[sc_r, rkv_r], writes=[yp_r])
                P.op("pe", lambda: nc.tensor.matmul(yp[:, o:o + 256], lhsT=qx[:, h, :], rhs=RbS[:, j, h, :], start=False, stop=True),
                     reads=[qx_r, snap_r[j][h]], writes=[yp_r])
            c["ysb"], c["ysb_r"] = ysbr.next()
            P.op("act", lambda: nc.scalar.copy(out=c["ysb"][:, 0:512], in_=ypA[:]), reads=[ypA_r], writes=[c["ysb_r"]])
            P.op("act", lambda: nc.scalar.copy(out=c["ysb"][:, 512:1024], in_=ypB[:]), reads=[ypB_r], writes=[c["ysb_r"]])

        def stB(j):
            c = C[j]
            st, st_r = str_.next()
            c["mv"], c["mv_r"] = mvr.next()
            for h in range(4):
                P.op("dve", lambda: nc.vector.bn_stats(out=st[:, h, :], in_=c["ysb"][:, h * 256:(h + 1) * 256]),
                     reads=[c["ysb_r"]], writes=[st_r])
            for h in range(4):
                P.op("dve", lambda: nc.vector.bn_aggr(out=c["mv"][:, h, :], in_=st[:, h, :]), reads=[st_r], writes=[c["mv_r"]])

        def stC(j):
            c = C[j]
            ln_t, ln_r = lnr.next()
            P.op("act", lambda: nc.scalar.activation(out=ln_t[:], in_=c["mv"][:, :, 1], func=AF.Ln, bias=float(EPS), scale=1.0),
                 reads=[c["mv_r"]], writes=[ln_r])
            c["rs"], c["rs_r"] = rsr.next()
            P.op("act", lambda: nc.scalar.activation(out=c["rs"][:], in_=ln_t[:], func=AF.Exp, scale=-0.5),
                 reads=[ln_r], writes=[c["rs_r"]])

        def stD(j):
            c = C[j]
            c["nm"], c["nm_r"] = nmr_.next()
            P.op("dve", lambda: nc.vector.scalar_tensor_tensor(out=c["nm"][:], in0=c["mv"][:, :, 0], scalar=-1.0, in1=c["rs"][:],
                                                               op0=ALU.mult, op1=ALU.mult),
                 reads=[c["mv_r"], c["rs_r"]], writes=[c["nm_r"]])

        def stE(j):
            c = C[j]
            c["yn"], c["yn_r"] = ynr.next()
            for h in range(4):
                P.op("act", lambda: nc.scalar.activation(out=c["yn"][:, h * 256:(h + 1) * 256], in_=c["ysb"][:, h * 256:(h + 1) * 256],
                                                         func=AF.Identity, bias=c["nm"][:, h:h + 1], scale=c["rs"][:, h:h + 1]),
                     reads=[c["ysb_r"], c["nm_r"], c["rs_r"]], writes=[c["yn_r"]])

        def stF(j):
            c = C[j]
            c["yret"], c["yret_r"] = yretr.next()
            P.op("dve", lambda: nc.vector.tensor_tensor(out=c["yret"][:], in0=c["yn"][:], in1=c["gz"][:], op=ALU.mult),
                 reads=[c["yn_r"], c["gz_r"]], writes=[c["yret_r"]])

        def stG(j):
            c = C[j]
            c["tp"], c["tp_r"] = t_pr.next()
            for cc in range(8):
                P.op("pe", lambda: nc.tensor.transpose(out=c["tp"][:, cc * 128:(cc + 1) * 128],
                                                       in_=c["yret"][:, cc * 128:(cc + 1) * 128], identity=ident[:]),
                     reads=[c["yret_r"]], writes=[c["tp_r"]])

        def stH(j):
            c = C[j]
            P.op("act", lambda: nc.scalar.copy(out=YTS["t"][:, 0:8, j * 128:(j + 1) * 128],
                                               in_=c["tp"][:].rearrange("p (k t) -> p k t", k=8)),
                 reads=[c["tp_r"]], writes=[YTS["r"]])
            C[j] = None

        stages = [stA, stB, stC, stD, stE, stF, stG, stH]
        ns = len(stages)
        for step in range(NT_OWN + ns - 1):
            for k in reversed(range(ns)):
                j = step - k
                if 0 <= j < NT_OWN:
                    stages[k](j)
        P.barrier()
        stack.close()

    def phase_outproj(w2d, tiles, res_rows, dst_rows):
        stack = contextlib.ExitStack()
        actT = YTS["t"]
        act_rs = [YTS["r"]] * 16
        ws = WStream(stack, bw=512, nstg=1)
        ws.prefetch([(wsrc(w2d, 0, 512), 0, 512)])
        psr = Ring(nc, stack, "ps", [128, 512], F32, 4, psum=True)
        xrr = Ring(nc, stack, "xres", [128, 512], F32, 5)
        oor = Ring(nc, stack, "oo", [128, 512], F32, 4)
        blocks = [[(wsrc(w2d, c0, 512), 0, 512)] for c0 in range(0, D, 512)]

        work = [(bi, t) for bi in range(len(blocks)) for t in tiles]
        loaded = []

        def ensure(n):
            while len(loaded) < min(n, len(work)):
                bi, t = work[len(loaded)]
                xr_t, xr_r = xrr.next()
                P.dma("sp", xr_t[:], res_rows(t)[:, bi * 512:bi * 512 + 512], writes=[xr_r])
                loaded.append((xr_t, xr_r))

        def body(i, wb, wb_r):
            c0 = i * 512
            for tix, t in enumerate(tiles):
                if tix == len(tiles) // 2:
                    ws.mid()
                gidx = i * len(tiles) + tix
                ensure(gidx + 3)
                xr_t, xr_r = loaded[gidx]
                ps, ps_r = psr.next()
                for k in range(16):
                    P.op("pe", lambda: nc.tensor.matmul(ps[:], lhsT=actT[:, k, t * 128:(t + 1) * 128], rhs=wb[:, k, :],
                                                        start=(k == 0), stop=(k == 15)),
                         reads=[wb_r[k], act_rs[k]], writes=[ps_r])
                oo, oo_r = oor.next()
                P.op("dve", lambda: nc.vector.tensor_tensor(out=oo[:], in0=ps[:], in1=xr_t[:], op=ALU.add),
                     reads=[ps_r, xr_r], writes=[oo_r])
                P.dma("sp", dst_rows(t)[:, c0:c0 + 512], oo[:], reads=[oo_r])

        run_blocks(ws, blocks, body, pre=True)
        P.barrier()
        stack.close()

    def phase_ple(li, tiles, src_rows, dst_rows):
        stack = contextlib.ExitStack()
        actT = stack.enter_context(sbuf(nc, "actT", [128, 16, TOK_OWN], BF16))
        actT_r = Res("actT")
        pT = stack.enter_context(sbuf(nc, "pT", [128, 2, TOK_OWN], BF16))
        pT_r = Res("pT")
        ntl = len(tiles)
        wp = stack.enter_context(sbuf(nc, "wp", [128, 2, D], BF16))
        wp_r = Res("wp")
        P.dma("pool", wp[:], w_ple[li].rearrange("(k p) c -> p k c", p=128), writes=[wp_r])
        ws = WStream(stack, bw=512, nstg=1, gidx=1 + 2 * li)
        ws.prefetch([(wsrc(w_gate[li], 0, 512), 0, 512)])
        with contextlib.ExitStack() as s1:
            ppr = Ring(nc, s1, "pp", [128, 256], F32, 2)
            pbr_ = Ring(nc, s1, "ppb", [128, 256], BF16, 2)
            pps = Ring(nc, s1, "ppps", [128, 256], BF16, 1, psum=True)

            def extra(ti):
                t = tiles[ti]
                pp, pp_r = ppr.next()
                P.dma("sp", pp[:], pin[li, t * 128:(t + 1) * 128, :], writes=[pp_r])
                pb, pb_r = pbr_.next()
                P.op("pool", lambda: nc.gpsimd.tensor_copy(out=pb[:], in_=pp[:]), reads=[pp_r], writes=[pb_r])
                tp, tp_r = pps.next()
                for c in range(2):
                    P.op("pe", lambda: nc.tensor.transpose(out=tp[:, c * 128:(c + 1) * 128], in_=pb[:, c * 128:(c + 1) * 128],
                                                           identity=ident[:]), reads=[pb_r], writes=[tp_r])
                P.op("dve", lambda: nc.vector.tensor_copy(out=pT[:, :, ti * 128:(ti + 1) * 128],
                                                          in_=tp[:].rearrange("p (k t) -> p k t", k=2)),
                     reads=[tp_r], writes=[pT_r])

            norm_transpose(s1, lambda ti: src_rows(tiles[ti]), ntl, 1 + 2 * li, actT, actT_r, extra=extra, nx=4, nxs=3)
        P.barrier()
        psr = Ring(nc, stack, "ps", [128, 512], F32, 3, psum=True)
        ps2r = Ring(nc, stack, "ps2", [128, 512], F32, 3, psum=True)
        xrr = Ring(nc, stack, "xres", [128, 512], F32, 4)
        ggr = Ring(nc, stack, "gg", [128, 512], F32, 2)
        oor = Ring(nc, stack, "oo", [128, 512], F32, 3)
        blocks = [[(wsrc(w_gate[li], c0, 512), 0, 512)] for c0 in range(0, D, 512)]

        work = [(bi, t) for bi in range(len(blocks)) for t in tiles]
        loaded = []

        def ensure(n):
            while len(loaded) < min(n, len(work)):
                bi, t = work[len(loaded)]
                xr_t, xr_r = xrr.next()
                P.dma("sp", xr_t[:], src_rows(t)[:, bi * 512:bi * 512 + 512], writes=[xr_r])
                loaded.append((xr_t, xr_r))

        def body(i, wb, wb_r):
            c0 = i * 512
            for ti, t in enumerate(tiles):
                if ti == len(tiles) // 2:
                    ws.mid()
                gidx = i * len(tiles) + ti
                ensure(gidx + 3)
                xr_t, xr_r = loaded[gidx]
                ps, ps_r = psr.next()
                for k in range(16):
                    P.op("pe", lambda: nc.tensor.matmul(ps[:], lhsT=actT[:, k, ti * 128:(ti + 1) * 128], rhs=wb[:, k, :],
                                                        start=(k == 0), stop=(k == 15)),
                         reads=[wb_r[k], actT_r], writes=[ps_r])
                p2, p2_r = ps2r.next()
                for k in range(2):
                    P.op("pe", lambda: nc.tensor.matmul(p2[:], lhsT=pT[:, k, ti * 128:(ti + 1) * 128], rhs=wp[:, k, c0:c0 + 512],
                                                        start=(k == 0), stop=(k == 1)),
                         reads=[wp_r, pT_r], writes=[p2_r])
                gg, gg_r = ggr.next()
                P.op("act", lambda: nc.scalar.activation(out=gg[:], in_=ps[:], func=AF.Sigmoid), reads=[ps_r], writes=[gg_r])
                oo, oo_r = oor.next()
                P.op("dve", lambda: nc.vector.tensor_tensor(out=oo[:], in0=p2[:], in1=gg[:], op=ALU.mult),
                     reads=[p2_r, gg_r], writes=[oo_r])
                P.op("dve", lambda: nc.vector.tensor_tensor(out=oo[:], in0=oo[:], in1=xr_t[:], op=ALU.add),
                     reads=[oo_r, xr_r], writes=[oo_r])
                P.dma("sp", dst_rows(t)[:, c0:c0 + 512], oo[:], reads=[oo_r])

        run_blocks(ws, blocks, body, pre=True)
        P.barrier()
        stack.close()

    def phase_inproj1():
        stack = contextlib.ExitStack()
        actT = stack.enter_context(sbuf(nc, "actT", [128, 16, TOK_OWN], BF16))
        actT_r = Res("actT")
        ws = WStream(stack, gidx=2, nstg=1)
        wv = w_in1.rearrange("(k p) (j c) -> p k j c", p=128, j=4)
        blocks = []
        for cc in range(16):
            blocks.append([(wv[:, :, 1, cc * 128:(cc + 1) * 128], 0, 128), (wv[:, :, 2, cc * 128:(cc + 1) * 128], 128, 128)])
            blocks.append([(wv[:, :, 0, cc * 128:(cc + 1) * 128], 0, 128), (wv[:, :, 3, cc * 128:(cc + 1) * 128], 128, 128)])
        ws.prefetch(blocks[0])
        with contextlib.ExitStack() as s1:
            norm_transpose(s1, lambda t: X2[t * 128:(t + 1) * 128, :], NT_OWN, 2, actT, actT_r, nx=3, nxs=3)
        P.barrier()
        cw = stack.enter_context(sbuf(nc, "convw_s", [128, 16, 3], F32))
        cw_r = Res("cw")
        P.dma("sp", cw[:], convw[:, :, :], writes=[cw_r])
        cu = stack.enter_context(sbuf(nc, "cu", [128, TOK_OWN + 2], F32))
        cu_r = Res("cu")
        cv = stack.enter_context(sbuf(nc, "cv", [128, TOK_OWN], F32))
        cv_r = Res("cv")
        P.op("dve", lambda: nc.vector.memset(cu[:, 0:2], 0.0), writes=[cu_r])
        pa = Ring(nc, stack, "psa", [128, 512], F32, 4, psum=True)
        pb_ = Ring(nc, stack, "psb", [128, 512], F32, 4, psum=True)
        usr = Ring(nc, stack, "us", [128, 512], F32, 2)
        szr = Ring(nc, stack, "sz", [128, 512], F32, 2)
        t1r = Ring(nc, stack, "t1", [128, 512], F32, 2)
        groups = [(0, 128)] + [(128 + 512 * g, 512) for g in range(4)]
        def mm(ps, ps_r, wb, wb_r, half, g0, gn):
            for k in range(16):
                P.op("pe", lambda: nc.tensor.matmul(ps[:, 0:gn], lhsT=wb[:, k, half * 128:(half + 1) * 128],
                                                    rhs=actT[:, k, g0:g0 + gn], start=(k == 0), stop=(k == 15)),
                     reads=[wb_r[k], actT_r], writes=[ps_r])

        def body(i, wb, wb_r):
            cc = i // 2
            if i % 2 == 0:
                for gi_, (g0, gn) in enumerate(groups):
                    if gi_ == 3:
                        ws.mid()
                    p_c, p_c_r = pa.next()
                    mm(p_c, p_c_r, wb, wb_r, 0, g0, gn)
                    p_u, p_u_r = pb_.next()
                    mm(p_u, p_u_r, wb, wb_r, 1, g0, gn)
                    us, us_r = usr.next()
                    P.op("act", lambda: nc.scalar.copy(out=us[:, 0:gn], in_=p_u[:, 0:gn]), reads=[p_u_r], writes=[us_r])
                    P.op("dve", lambda: nc.vector.tensor_tensor(out=cu[:, 2 + g0:2 + g0 + gn], in0=p_c[:, 0:gn], in1=us[:, 0:gn],
                                                                op=ALU.mult), reads=[p_c_r, us_r], writes=[cu_r])
                P.op("act", lambda: nc.scalar.activation(out=cv[:], in_=cu[:, 0:TOK_OWN], func=AF.Copy, scale=cw[:, cc, 0:1]),
                     reads=[cu_r, cw_r], writes=[cv_r])
                P.op("dve", lambda: nc.vector.scalar_tensor_tensor(out=cv[:], in0=cu[:, 1:TOK_OWN + 1], scalar=cw[:, cc, 1:2], in1=cv[:],
                                                                   op0=ALU.mult, op1=ALU.add), reads=[cu_r, cw_r, cv_r], writes=[cv_r])
                P.op("dve", lambda: nc.vector.scalar_tensor_tensor(out=cv[:], in0=cu[:, 2:TOK_OWN + 2], scalar=cw[:, cc, 2:3], in1=cv[:],
                                                                   op0=ALU.mult, op1=ALU.add), reads=[cu_r, cw_r, cv_r], writes=[cv_r])
            else:
                for gi_, (g0, gn) in enumerate(groups):
                    if gi_ == 3:
                        ws.mid()
                    if gi_ == 0:
                        continue
                    p_b, p_b_r = pa.next()
                    mm(p_b, p_b_r, wb, wb_r, 0, g0, gn)
                    p_z, p_z_r = pb_.next()
                    mm(p_z, p_z_r, wb, wb_r, 1, g0, gn)
                    sz, sz_r = szr.next()
                    P.op("act", lambda: nc.scalar.activation(out=sz[:, 0:gn], in_=p_z[:, 0:gn], func=AF.Silu),
                         reads=[p_z_r], writes=[sz_r])
                    t1, t1_r = t1r.next()
                    P.op("dve", lambda: nc.vector.tensor_tensor(out=t1[:, 0:gn], in0=p_b[:, 0:gn], in1=cv[:, g0:g0 + gn], op=ALU.mult),
                         reads=[p_b_r, cv_r], writes=[t1_r])
                    P.op("dve", lambda: nc.vector.tensor_tensor(out=YTS["t"][:, cc, g0:g0 + gn], in0=t1[:, 0:gn], in1=sz[:, 0:gn],
                                                                op=ALU.mult), reads=[t1_r, sz_r], writes=[YTS["r"]])

        run_blocks(ws, blocks, body, pre=True)
        P.barrier()
        stack.close()

    own_rows_x = lambda t: xin[(NT_CTX + t) * 128:(NT_CTX + t + 1) * 128, :]
    rows = lambda X: (lambda t: X[t * 128:(t + 1) * 128, :])
    all_tiles = list(range(NT_OWN))
    own_tiles = list(range(1, NT_OWN))
    YTS = {}

    def yT_open():
        cm = sbuf(nc, "yT_res", [128, 16, TOK_OWN], BF16)
        YTS["cm"] = cm
        YTS["t"] = cm.__enter__()
        YTS["r"] = Res("yT")

    def yT_close():
        YTS["cm"].__exit__(None, None, None)

    ph = 0
    steps = [
        lambda: phase_inproj0(0, NT_CTX, False),
        lambda: (rs_open(), phase_inproj0(NT_CTX, NT_OWN, True)),
        lambda: (yT_open(), phase_attention()),
        phase_retention,
        lambda: (phase_outproj(w_out0, all_tiles, own_rows_x, rows(X1)), yT_close(), rs_close()),
        lambda: phase_ple(0, all_tiles, rows(X1), rows(X2)),
        lambda: (yT_open(), phase_inproj1()),
        lambda: (phase_outproj(w_out1, own_tiles, rows(X2), rows(X3)), yT_close()),
        lambda: phase_ple(1, own_tiles, rows(X3), lambda t: out[(t - 1) * 128:t * 128, :]),
    ]
    for i, st in enumerate(steps):
        if i < upto:
            st()
    P.barrier()
    gstack.close()
    P.close()
    return nc


def _tables():
    slopes = (2.0 ** (-8.0 / 8)) ** np.arange(1, 9)
    k = np.arange(128)[:, None].astype(np.float64)
    q = np.arange(128)[None, :].astype(np.float64)
    et = np.zeros((8, 128, ETW), np.float32)
    for h in range(8):
        for pi, d in enumerate(PATTERNS):
            rel = q - k
            diag = np.where(rel >= 0, np.exp(-slopes[h] * np.maximum(rel, 0.0) * d), 0.0)
            relp = q - k + 128
            prev = np.where(relp <= 128, np.exp(-slopes[h] * relp * d), 0.0)
            o = pi * 512
            et[h, :, o:o + 128] = prev
            et[h, :, o + 128:o + 256] = diag
            et[h, :, o + 256:o + 384] = prev
            et[h, :, o + 384:o + 512] = diag
            if d == 4:
                for r in range(4):
                    et[h, :, 1536 + r * 64:1536 + r * 64 + 32] = prev[:, 96:128]
                    et[h, :, 1536 + r * 64 + 32:1536 + r * 64 + 64] = diag[:, 96:128]
            if d == 16:
                for r in range(16):
                    et[h, :, 1792 + r * 8:1792 + (r + 1) * 8] = diag[:, 120:128]
    rt = np.zeros((128, 8, 128), np.float32)
    zt = np.zeros((128, 4), np.float32)
    for h in range(4):
        lg = np.log(RET_GAMMA[h])
        diff = q - k
        rt[:, h, :] = np.where(diff >= 0, np.exp(lg * np.maximum(diff, 0.0)), 0.0) * 128 ** -0.5
        rt[:, 4 + h, :] = np.exp(lg * (q + 1.0)) * np.ones((128, 1))
        zt[:, h] = np.exp(lg * (127.0 - np.arange(128))) * 128 ** -0.5
    return et, rt, zt


def make_in_maps(x, p, pre_norm_g, w_in_even, q_norm_g, k_norm_g, ret_gn_g, w_out_even,
                 w_in_odd, conv_w_odd, w_out_odd, ple_norm_g, w_ple_gate, w_ple_proj):
    f = lambda a: np.ascontiguousarray(np.asarray(a, dtype=np.float32))
    x, p = f(x), f(p)
    et, rt, zt = _tables()
    et0 = et.copy()
    for pi in range(3):
        et0[:, :, pi * 512 + 256:pi * 512 + 384] = 0.0
    gcol = np.stack([f(g).reshape(16, 128).T for g in
                     (pre_norm_g[0], ple_norm_g[0], pre_norm_g[1], ple_norm_g[1])]).astype(np.float32)
    qkg = np.stack([f(q_norm_g)[0], f(k_norm_g)[0]], axis=1)
    gng = np.broadcast_to(f(ret_gn_g)[0].reshape(1, 1024), (128, 1024))
    convw = f(conv_w_odd)[0].reshape(3, 16, 128).transpose(2, 1, 0)
    shared = {
        "w_in0": f(w_in_even)[0], "w_out0": f(w_out_even)[0], "w_in1": f(w_in_odd)[0], "w_out1": f(w_out_odd)[0],
        "w_gate": f(w_ple_gate), "w_ple": f(w_ple_proj), "gcol": np.ascontiguousarray(gcol), "qkg": np.ascontiguousarray(qkg),
        "gng": np.ascontiguousarray(gng), "convw": np.ascontiguousarray(convw), "rtab": rt, "zeta": zt,
        "ident": np.eye(128, dtype=np.float32).astype(ml_dtypes.bfloat16),
        "ones": np.ones((128, 128), np.float32).astype(ml_dtypes.bfloat16),
    }
    maps = []
    for c in range(8):
        b, s = c // 2, c % 2
        if s == 1:
            xin = x[b]
            pin = p[:, b, TOK_CTX:T, :]
            e = et
        else:
            xin = np.concatenate([np.zeros((2048, D), np.float32), x[b, 0:2048]], axis=0)
            pin = np.concatenate([np.zeros((2, 128, 256), np.float32), p[:, b, 0:2048, :]], axis=1)
            e = et0
        m = dict(shared)
        m["xin"] = np.ascontiguousarray(xin)
        m["pin"] = np.ascontiguousarray(pin)
        m["etab"] = e
        maps.append(m)
    return maps


def kernel(**inputs):
    maps = make_in_maps(**inputs)
    nc = build()
    res = run_bass_kernel_spmd(nc, maps, core_ids=list(range(8)))
    out = np.zeros((4, T, D), np.float32)
    for c in range(8):
        b, s = c // 2, c % 2
        out[b, s * 2048:(s + 1) * 2048, :] = res.results[c]["out"]
    return out
```
